# Optimizing a Trainium2 kernel written in Bass

```python
import math
import jax, jax.numpy as jnp
from jax import lax
import numpy as np

D_MODEL = 1024
BATCH = 32
SEQ = 2048
DEPTH = 4

GRID_W = 64
CTX_LEN = 256
N_MOD = 9
D_FF = 2816
CONV_CH = D_MODEL
CONV_K = 31
DN_DK = 128
DN_DV = 128
DN_HEADS = D_MODEL // 128
DN_HK = DN_HEADS * DN_DK
DN_HV = DN_HEADS * DN_DV
SHORT_K = 3
CHUNK = 64
EPS = 1e-6

O_CONV = 0
O_QKV = O_CONV + 2 * CONV_CH
O_Z = O_QKV + 2 * DN_HK + DN_HV
O_AB = O_Z + DN_HV
O_GATE = O_AB + 4 * DN_HEADS
N_IN = O_GATE + 2 * D_MODEL

kernel_name = "hybrid_conformer_gdn_prefix_dit"


def _rms(x, g):
    xf = x.astype(jnp.float32)
    y = xf * lax.rsqrt(jnp.mean(xf * xf, axis=-1, keepdims=True) + EPS)
    return y.astype(x.dtype) * g


def _layernorm(x, g, b):
    xf = x.astype(jnp.float32)
    mu = jnp.mean(xf, axis=-1, keepdims=True)
    var = jnp.mean(jnp.square(xf - mu), axis=-1, keepdims=True)
    return ((xf - mu) * lax.rsqrt(var + EPS)).astype(x.dtype) * g + b


def _l2norm(x):
    xf = x.astype(jnp.float32)
    return xf * lax.rsqrt(jnp.sum(xf * xf, axis=-1, keepdims=True) + EPS)


def _modulate(n, shift, scale):
    return n * (1.0 + scale) + shift


def _swiglu(x, w13, w2):
    a, b = jnp.split(x @ w13, 2, axis=-1)
    return (jax.nn.silu(a) * b) @ w2


def _dwconv(x, w):
    k = w.shape[0]
    return lax.conv_general_dilated(
        x, w[:, None, :].astype(x.dtype), window_strides=(1,), padding=[(k // 2, k // 2)],
        dimension_numbers=("NWC", "WIO", "NWC"), feature_group_count=x.shape[-1])


def _conformer_conv(u, rows, dw, dw_b, ln_g, ln_b, proj):
    a, gate = jnp.split(u, 2, axis=-1)
    y = a * jax.nn.sigmoid(gate)
    b, t, ch = y.shape
    y = _dwconv(y.reshape(b * rows, t // rows, ch), dw).reshape(b, t, ch) + dw_b
    y = jax.nn.silu(_layernorm(y, ln_g, ln_b))
    return y @ proj


def _dn_inputs(p_qkv, p_ab, short_w, a_log, dt_bias):
    b, t, _ = p_qkv.shape
    qkv = jax.nn.silu(_dwconv(p_qkv, short_w))
    q, k, v = jnp.split(qkv, [DN_HK, 2 * DN_HK], axis=-1)
    q = _l2norm(q.reshape(b, t, DN_HEADS, DN_DK)) * (DN_DK ** -0.5)
    k = _l2norm(k.reshape(b, t, DN_HEADS, DN_DK))
    v = v.reshape(b, t, DN_HEADS, DN_DV).astype(jnp.float32)
    ab = p_ab.astype(jnp.float32).reshape(b, t, 2, 2, DN_HEADS)
    g = -jnp.exp(a_log.astype(jnp.float32)) * jax.nn.softplus(ab[:, :, 0] + dt_bias.astype(jnp.float32))
    beta = jax.nn.sigmoid(ab[:, :, 1])
    return q, k, v, g, beta


def _chunk_gdr(q, k, v, g, beta, s0):
    b, t, h, dk = q.shape
    dv = v.shape[-1]
    n = t // CHUNK

    def blocks(a):
        return jnp.moveaxis(a.reshape(b, n, CHUNK, h, *a.shape[3:]), 2, 3)

    q, k, v, g, beta = (blocks(a) for a in (q, k, v, g, beta))
    g = jnp.cumsum(g, axis=-1)
    idx = jnp.arange(CHUNK)
    incl = idx[:, None] >= idx[None, :]
    strict = idx[:, None] > idx[None, :]
    decay = jnp.exp(jnp.where(incl, g[..., :, None] - g[..., None, :], -jnp.inf))
    kb = k * beta[..., None]
    a_mat = jnp.einsum("bnhid,bnhjd->bnhij", kb, k) * decay
    lower = jnp.where(strict, a_mat, 0.0) + jnp.eye(CHUNK, dtype=a_mat.dtype)
    rhs = jnp.concatenate([v * beta[..., None], kb * jnp.exp(g)[..., None]], axis=-1)
    sol = lax.linalg.triangular_solve(lower, rhs, left_side=True, lower=True, unit_diagonal=True)
    u, w = sol[..., :dv], sol[..., dv:]
    qk = jnp.einsum("bnhid,bnhjd->bnhij", q, k) * decay

    def step(s, blk):
        qb, kbk, ub, wb, gb, qkb = blk
        v_new = ub - jnp.einsum("bhck,bhkv->bhcv", wb, s)
        o = (jnp.einsum("bhck,bhkv->bhcv", qb * jnp.exp(gb)[..., None], s)
             + jnp.einsum("bhij,bhjv->bhiv", qkb, v_new))
        g_last = gb[..., -1:]
        s = (s * jnp.exp(g_last)[..., None]
             + jnp.einsum("bhck,bhcv->bhkv", kbk * jnp.exp(g_last - gb)[..., None], v_new))
        return s, o

    xs = tuple(jnp.moveaxis(a, 1, 0) for a in (q, k, u, w, g, qk))
    s_final, o = lax.scan(step, s0, xs)
    o = jnp.moveaxis(jnp.moveaxis(o, 0, 1), 2, 3).reshape(b, t, h, dv)
    return o, s_final


def _scan_dir(q, k, v, g, beta, s0, reverse):
    if reverse:
        q, k, v, g, beta = (jnp.flip(a, axis=1) for a in (q, k, v, g, beta))
    o, s = _chunk_gdr(q, k, v, g, beta, s0)
    return (jnp.flip(o, axis=1) if reverse else o), s


def _merge_out(p, o, rows, conv_dw, conv_dw_b, conv_ln_g, conv_ln_b, conv_proj, dn_onorm, dn_proj, w_out):
    b, t, _ = p.shape
    y_conv = _conformer_conv(p[..., O_CONV:O_QKV], rows, conv_dw, conv_dw_b, conv_ln_g, conv_ln_b, conv_proj)
    z = p[..., O_Z:O_AB].reshape(b, t, DN_HEADS, DN_DV)
    o = _rms(o.astype(p.dtype), dn_onorm) * jax.nn.silu(z)
    y_dn = o.reshape(b, t, DN_HV) @ dn_proj
    g_conv, g_dn = jnp.split(p[..., O_GATE:N_IN], 2, axis=-1)
    return (jax.nn.sigmoid(g_conv) * y_conv + jax.nn.sigmoid(g_dn) * y_dn) @ w_out


def _mixer(n_lat, n_ctx, rows, w_in, conv_dw, conv_dw_b, conv_ln_g, conv_ln_b, conv_proj,
           dn_short, dn_a_log, dn_dt_bias, dn_onorm, dn_proj, w_out, need_ctx_out):
    p_lat = n_lat @ w_in
    p_ctx = n_ctx @ w_in
    ql, kl, vl, gl, bl = _dn_inputs(p_lat[..., O_QKV:O_Z], p_lat[..., O_AB:O_GATE], dn_short, dn_a_log, dn_dt_bias)
    qc, kc, vc, gc, bc = _dn_inputs(p_ctx[..., O_QKV:O_Z], p_ctx[..., O_AB:O_GATE], dn_short, dn_a_log, dn_dt_bias)
    s0 = jnp.zeros((n_ctx.shape[0], DN_HEADS, DN_DK, DN_DV), jnp.float32)
    o_lat = 0.0
    o_ctx = 0.0
    for d, rev in enumerate((False, True)):
        oc, s_ctx = _scan_dir(qc, kc, vc, gc[:, :, d], bc[:, :, d], s0, rev)
        ol, _ = _scan_dir(ql, kl, vl, gl[:, :, d], bl[:, :, d], s_ctx, rev)
        o_lat = o_lat + ol
        o_ctx = o_ctx + oc
    y_lat = _merge_out(p_lat, o_lat, rows, conv_dw, conv_dw_b, conv_ln_g, conv_ln_b, conv_proj, dn_onorm, dn_proj, w_out)
    if not need_ctx_out:
        return y_lat, None
    y_ctx = _merge_out(p_ctx, o_ctx, 1, conv_dw, conv_dw_b, conv_ln_g, conv_ln_b, conv_proj, dn_onorm, dn_proj, w_out)
    return y_lat, y_ctx


def setup_inputs(seed: int = 0) -> dict:
    key = jax.random.key(seed)
    ks = jax.random.split(key, 32)

    def nrm(k, shape, scale=1.0):
        return scale * jax.random.normal(k, shape, jnp.float32)

    dt = jnp.exp(jax.random.uniform(ks[20], (DEPTH, 2, DN_HEADS), jnp.float32, math.log(1e-3), math.log(1e-1)))
    return {
        "x": nrm(ks[0], (BATCH, SEQ, D_MODEL)),
        "c": nrm(ks[1], (BATCH, D_MODEL)),
        "ctx": nrm(ks[2], (BATCH, CTX_LEN, D_MODEL)),
        "c_ctx": nrm(ks[3], (D_MODEL,)),
        "ada_w": nrm(ks[4], (DEPTH, D_MODEL, N_MOD * D_MODEL), 0.5 * D_MODEL ** -0.5),
        "ada_b": nrm(ks[5], (DEPTH, N_MOD * D_MODEL), 0.02),
        "ffn1_norm": 1.0 + nrm(ks[6], (DEPTH, D_MODEL), 0.1),
        "ffn1_w13": nrm(ks[7], (DEPTH, D_MODEL, 2 * D_FF), D_MODEL ** -0.5),
        "ffn1_w2": nrm(ks[8], (DEPTH, D_FF, D_MODEL), D_FF ** -0.5),
        "mix_norm": 1.0 + nrm(ks[9], (DEPTH, D_MODEL), 0.1),
        "w_in": nrm(ks[10], (DEPTH, D_MODEL, N_IN), D_MODEL ** -0.5),
        "conv_dw": nrm(ks[11], (DEPTH, CONV_K, CONV_CH), CONV_K ** -0.5),
        "conv_dw_b": nrm(ks[12], (DEPTH, CONV_CH), 0.02),
        "conv_ln_g": 1.0 + nrm(ks[13], (DEPTH, CONV_CH), 0.1),
        "conv_ln_b": nrm(ks[14], (DEPTH, CONV_CH), 0.02),
        "conv_proj": nrm(ks[15], (DEPTH, CONV_CH, D_MODEL), CONV_CH ** -0.5),
        "dn_short": nrm(ks[16], (DEPTH, SHORT_K, 2 * DN_HK + DN_HV), SHORT_K ** -0.5),
        "dn_a_log": jnp.log(jax.random.uniform(ks[17], (DEPTH, 2, DN_HEADS), jnp.float32, 1.0, 16.0)),
        "dn_dt_bias": dt + jnp.log(-jnp.expm1(-dt)),
        "dn_onorm": 1.0 + nrm(ks[18], (DEPTH, DN_DV), 0.1),
        "dn_proj": nrm(ks[19], (DEPTH, DN_HV, D_MODEL), DN_HV ** -0.5),
        "w_out": nrm(ks[21], (DEPTH, D_MODEL, D_MODEL), D_MODEL ** -0.5),
        "ffn2_norm": 1.0 + nrm(ks[22], (DEPTH, D_MODEL), 0.1),
        "ffn2_w13": nrm(ks[23], (DEPTH, D_MODEL, 2 * D_FF), D_MODEL ** -0.5),
        "ffn2_w2": nrm(ks[24], (DEPTH, D_FF, D_MODEL), D_FF ** -0.5),
        "final_norm": 1.0 + nrm(ks[25], (D_MODEL,), 0.1),
    }


def reference(x, c, ctx, c_ctx, ada_w, ada_b, ffn1_norm, ffn1_w13, ffn1_w2, mix_norm, w_in,
              conv_dw, conv_dw_b, conv_ln_g, conv_ln_b, conv_proj, dn_short, dn_a_log, dn_dt_bias,
              dn_onorm, dn_proj, w_out, ffn2_norm, ffn2_w13, ffn2_w2, final_norm):
    rows = x.shape[1] // GRID_W
    h = x
    hc = ctx
    for l in range(DEPTH):
        last = l == DEPTH - 1
        ml = jnp.split((jax.nn.silu(c) @ ada_w[l] + ada_b[l])[:, None, :], N_MOD, axis=-1)
        mc = jnp.split(jax.nn.silu(c_ctx) @ ada_w[l] + ada_b[l], N_MOD, axis=-1)
        h = h + 0.5 * ml[2] * _swiglu(_modulate(_rms(h, ffn1_norm[l]), ml[0], ml[1]), ffn1_w13[l], ffn1_w2[l])
        hc = hc + 0.5 * mc[2] * _swiglu(_modulate(_rms(hc, ffn1_norm[l]), mc[0], mc[1]), ffn1_w13[l], ffn1_w2[l])
        y_lat, y_ctx = _mixer(
            _modulate(_rms(h, mix_norm[l]), ml[3], ml[4]), _modulate(_rms(hc, mix_norm[l]), mc[3], mc[4]), rows,
            w_in[l], conv_dw[l], conv_dw_b[l], conv_ln_g[l], conv_ln_b[l], conv_proj[l],
            dn_short[l], dn_a_log[l], dn_dt_bias[l], dn_onorm[l], dn_proj[l], w_out[l], not last)
        h = h + ml[5] * y_lat
        h = h + 0.5 * ml[8] * _swiglu(_modulate(_rms(h, ffn2_norm[l]), ml[6], ml[7]), ffn2_w13[l], ffn2_w2[l])
        if not last:
            hc = hc + mc[5] * y_ctx
            hc = hc + 0.5 * mc[8] * _swiglu(_modulate(_rms(hc, ffn2_norm[l]), mc[6], mc[7]), ffn2_w13[l], ffn2_w2[l])
    return _rms(h, final_norm)
```

```python
from contextlib import ExitStack
import numpy as np
import concourse.bass as bass
import concourse.mybir as mybir
from concourse.bass_utils import run_bass_kernel_spmd

F32 = mybir.dt.float32
BF16 = mybir.dt.bfloat16
ALU = mybir.AluOpType
AF = mybir.ActivationFunctionType
AX = mybir.AxisListType

COMPUTE = ("tensor", "vector", "scalar", "gpsimd")


class Buf:
    __slots__ = ("name", "w", "r", "dkey")

    def __init__(self, name):
        self.name = name
        self.w = {}
        self.r = {}
        self.dkey = None


class Op:
    __slots__ = ("eng", "fn", "deps", "key", "val", "needed", "is_dma")


class Sched:
    def __init__(self, nc, stack):
        self.nc = nc
        self.ops = []
        self.nd = 0
        self.flushed = 0
        self.sems = {}
        self.cnt = {}
        self.seen = {e: {} for e in COMPUTE + ("sync",)}
        self.lastop = {}
        self.bar = 0
        self._stack = stack
        self.free_dkeys = []

    def _sem(self, key):
        if key not in self.sems:
            self.sems[key] = self._stack.enter_context(self.nc.semaphore("s_" + str(key)))
            self.cnt[key] = 0
        return self.sems[key]

    def _add(self, eng, fn, reads, writes, parts, key, is_dma):
        op = Op()
        op.eng = eng
        op.fn = fn
        op.key = key
        op.is_dma = is_dma
        op.needed = is_dma
        op.val = None
        oid = len(self.ops)
        deps = {}
        own = eng if eng in COMPUTE else None
        bar = self.bar
        for b in reads:
            for k, o in b.w.items():
                if k == own and eng == "tensor":
                    continue
                if o >= bar and deps.get(k, -1) < o:
                    deps[k] = o
        for b in writes:
            for k, o in b.w.items():
                if k == own or k == key:
                    continue
                if o >= bar and deps.get(k, -1) < o:
                    deps[k] = o
            for k, o in b.r.items():
                if k == own:
                    continue
                if o >= bar and deps.get(k, -1) < o:
                    deps[k] = o
        for b in parts:
            for k, o in b.r.items():
                if k == own:
                    continue
                if o >= bar and deps.get(k, -1) < o:
                    deps[k] = o
        op.deps = deps
        for o in deps.values():
            self.ops[o].needed = True
        self.ops.append(op)
        for b in reads:
            b.r[key] = oid
        for b in writes:
            b.w[key] = oid
        for b in parts:
            b.w[key] = oid
        self.lastop[key] = oid
        return oid

    def op(self, eng, fn, reads=(), writes=(), parts=()):
        self._sem(eng)
        return self._add(eng, fn, reads, writes, parts, eng, False)

    def dma(self, eng, out_ap, in_ap, reads=(), writes=(), parts=(), sem_buf=None):
        if sem_buf is None:
            sem_buf = (list(writes) + list(parts))[0]
        if sem_buf.dkey is None:
            if self.free_dkeys:
                sem_buf.dkey = self.free_dkeys.pop()
            else:
                sem_buf.dkey = "d%d" % self.nd
                self.nd += 1
        self._sem(sem_buf.dkey)

        def fn(e, o=out_ap, i=in_ap):
            return e.dma_start(out=o, in_=i)
        return self._add(eng, fn, reads, writes, parts, sem_buf.dkey, True)

    def release(self, bufs):
        for b in bufs:
            if b.dkey is not None:
                self.free_dkeys.append(b.dkey)
                b.dkey = None

    def barrier(self):
        allk = dict(self.lastop)
        for eng in COMPUTE + ("sync",):
            op = Op()
            op.eng = eng
            op.fn = None
            op.key = None
            op.is_dma = False
            op.needed = False
            op.val = None
            op.deps = {k: o for k, o in allk.items() if k != eng and o >= self.bar}
            for o in op.deps.values():
                self.ops[o].needed = True
            self.ops.append(op)
        self.bar = len(self.ops)

    def flush(self):
        ops = self.ops[self.flushed:]
        self.flushed = len(self.ops)
        for op in ops:
            if op.fn is None:
                continue
            if op.is_dma:
                self.cnt[op.key] += 16
                op.val = self.cnt[op.key]
            elif op.needed:
                self.cnt[op.key] += 1
                op.val = self.cnt[op.key]
        streams = {e: [] for e in COMPUTE + ("sync",)}
        for op in ops:
            streams[op.eng].append(op)
        allops = self.ops
        sems = self.sems
        seen = self.seen

        def run(eng_name, e):
            sn = seen[eng_name]
            for op in streams[eng_name]:
                for k, o in op.deps.items():
                    v = allops[o].val
                    assert v is not None, (eng_name, k, o)
                    if sn.get(k, 0) < v:
                        e.wait_ge(sems[k], v)
                        sn[k] = v
                if op.fn is None:
                    continue
                ins = op.fn(e)
                if op.val is not None:
                    ins.then_inc(sems[op.key], 16 if op.is_dma else 1)

        with self.nc.Block() as block:
            if streams["sync"]:
                @block.sync
                def _(e):
                    run("sync", e)
            if streams["tensor"]:
                @block.tensor
                def _(e):
                    run("tensor", e)
            if streams["vector"]:
                @block.vector
                def _(e):
                    run("vector", e)
            if streams["scalar"]:
                @block.scalar
                def _(e):
                    run("scalar", e)
            if streams["gpsimd"]:
                @block.gpsimd
                def _(e):
                    run("gpsimd", e)
        for op in ops:
            op.fn = None if op.fn is None else True


D = 1024
KC = 8
SEQ = 2048
CTXL = 256
T = SEQ + CTXL
NCH = T // 64
DEPTH = 4
NCORES = 8
BATCH = 32
DFF = 2816
NJ = DFF // 128
NIN = 8224
O_CONV, O_QKV, O_Z, O_AB, O_GATE = 0, 2048, 5120, 6144, 6176
TN = 256
NT = T // TN
EPS = 1e-6
TP = T + 3

V_N1, V_NM, V_N2, V_CB, V_LG, V_LB, V_ON, V_DW, V_SH = 0, 8, 16, 24, 32, 40, 48, 49, 49 + 248
NV = 49 + 248 + 72


def ptok(c):
    return 1 + 64 * c if c < 4 else 2 + 64 * c


class Ctx:
    pass


_UID = [0]


def _sbt(nc, name, shape, dt):
    _UID[0] += 1
    return nc.sbuf_tensor("%s_%d" % (name, _UID[0]), shape, dt)


def _pst(nc, name, shape, dt):
    _UID[0] += 1
    return nc.psum_tensor("%s_%d" % (name, _UID[0]), shape, dt)


def build_program(L=DEPTH, NBL=4, stop_after=None, debug=False, only=None, dn_phase=None):
    nc = bass.Bass("TRN2", target_bir_lowering=False)
    dk = "ExternalOutput" if debug else "Internal"

    def din(name, shape, dt=F32):
        if only == "dn" and name not in ("vecs", "fin", "dnc", "ident", "cmask"):
            return None
        return nc.dram_tensor(name, list(shape), dt, kind="ExternalInput").ap()

    g = Ctx()
    g.nc = nc
    g.L, g.NBL = L, NBL
    g.hT0 = din("hT0", [NBL, 128, KC, T])
    g.cT = din("cT", [128, KC, 5])
    g.adaw = din("adaw", [L, 128, KC, 9 * D])
    g.adab = din("adab", [L, 128, 72])
    g.vecs_d = din("vecs", [128, L, NV])
    g.fin_d = din("fin", [128, KC])
    g.dnc_d = din("dnc", [64, L, 2, 16])
    g.w13 = [din("w13a", [L, 128, KC, 2 * DFF]), din("w13b", [L, 128, KC, 2 * DFF])]
    g.w2 = [din("w2a", [L, 128, NJ, D]), din("w2b", [L, 128, NJ, D])]
    g.win = din("win", [L, 128, KC, NIN])
    g.cproj = din("cproj", [L, 128, KC, D])
    g.dproj = din("dproj", [L, 128, KC, D])
    g.wout = din("wout", [L, 128, KC, D])
    g.ident_d = din("ident", [128, 128])
    g.cm_d = din("cmask", [64, 10, 64])
    g.outT = nc.dram_tensor("outT", [NBL, 128, KC, SEQ], F32, kind="ExternalOutput").ap()
    g.H = nc.dram_tensor("Hs", [NBL, 128, KC, T], F32, kind=dk).ap()
    g.PQ = nc.dram_tensor("PQs", [NBL, 24, 128, T], F32, kind="ExternalInput" if only == "dn" else dk).ap()
    g.AB = nc.dram_tensor("ABs", [NBL, T, 32], F32, kind="ExternalInput" if only == "dn" else dk).ap()
    g.dn_phase = dn_phase
    g.OT = nc.dram_tensor("OTs", [NBL, 128, KC, T], BF16, kind=dk).ap()
    g.MODS = nc.dram_tensor("MODs", [128, L, 72, 5], F32, kind=dk).ap()
    g.YC = nc.dram_tensor("YCs", [NBL, 128, KC, T], BF16, kind=dk).ap()

    with ExitStack() as stack:
        S = Sched(nc, stack)
        g.S = S
        g.stack = stack
        sb = lambda n, s, d=F32: stack.enter_context(_sbt(nc, n, list(s), d))
        g.mods = sb("mods", [128, L, 72, 5])
        g.vecs = sb("vecs_sb", [128, L, NV])
        g.fin = sb("fin_sb", [128, KC])
        g.dnc = sb("dnc_sb", [64, L, 2, 16])
        g.ones_bf = sb("ones_bf", [128, 128], BF16)
        g.ones_f = sb("ones_f", [64, 128])
        g.ident = sb("ident_bf", [128, 128], BF16)
        g.cm = sb("cm_sb", [64, 10, 64])
        g.cvals = sb("cvals", [128, 4])
        g.b_const = Buf("const")
        cb = g.b_const
        S.dma("sync", g.vecs[:], g.vecs_d, parts=[cb])
        S.dma("sync", g.fin[:], g.fin_d, parts=[cb])
        S.dma("sync", g.dnc[:], g.dnc_d, parts=[cb])
        S.dma("sync", g.cm[:], g.cm_d, parts=[cb])
        S.dma("gpsimd", g.ident[:], g.ident_d, parts=[cb])
        S.op("gpsimd", lambda e: e.memset(g.ones_bf[:], 1.0), parts=[cb])
        S.op("gpsimd", lambda e: e.memset(g.ones_f[:], 1.0), parts=[cb])
        S.op("gpsimd", lambda e: e.memset(g.cvals[:, 0:1], EPS), parts=[cb])
        S.op("gpsimd", lambda e: e.memset(g.cvals[:, 1:2], 4.0 * EPS), parts=[cb])
        S.op("gpsimd", lambda e: e.memset(g.cvals[:, 2:3], 1.0), parts=[cb])
        S.op("gpsimd", lambda e: e.memset(g.cvals[:, 3:4], 0.0), parts=[cb])
        S.barrier()
        S.flush()

        stages = []
        if only == "dn":
            stage_dn(g, 0)
            S.barrier()
            S.flush()
            return nc
        stages.append(("mods", lambda: stage_mods(g)))
        for l in range(L):
            last = l == L - 1
            stages.append(("ffn1_%d" % l, lambda l=l: stage_ffn(g, l, 0, True)))
            stages.append(("m1_%d" % l, lambda l=l: stage_m1(g, l)))
            stages.append(("conv_%d" % l, lambda l=l, last=last: stage_conv(g, l, not last)))
            stages.append(("dn_%d" % l, lambda l=l: stage_dn(g, l)))
            stages.append(("m3_%d" % l, lambda l=l, last=last: stage_m3(g, l, not last)))
            stages.append(("ffn2_%d" % l, lambda l=l, last=last: stage_ffn(g, l, 1, not last)))
        stages.append(("final", lambda: stage_final(g)))
        for name, fn in stages:
            fn()
            if stop_after == name:
                break
        S.barrier()
        S.flush()
    return nc


def mm(S, out, lhsT, rhs, start, stop, R, W):
    S.op("tensor", lambda e: e.matmul(out, lhsT=lhsT, rhs=rhs, start=start, stop=stop), R, W)


def tr(S, out, in_, ident, R, W):
    S.op("tensor", lambda e: e.matmul(out, lhsT=in_, rhs=ident, start=True, stop=True), R, W)


def act(S, out, in_, func, R, W, scale=None, bias=None):
    kw = {}
    if scale is not None:
        kw["scale"] = scale
    if bias is not None:
        kw["bias"] = bias
    S.op("scalar", lambda e: e.activation(out=out, in_=in_, func=func, **kw), R, W)


def tt(S, eng, out, in0, in1, op, R, W):
    S.op(eng, lambda e: e.tensor_tensor(out=out, in0=in0, in1=in1, op=op), R, W)


def ts(S, eng, out, in0, s1, s2, op0, op1, R, W):
    if s2 is None:
        S.op(eng, lambda e: e.tensor_scalar(out=out, in0=in0, scalar1=s1, scalar2=None, op0=op0), R, W)
    else:
        S.op(eng, lambda e: e.tensor_scalar(out=out, in0=in0, scalar1=s1, scalar2=s2, op0=op0, op1=op1), R, W)


def stt(S, out, in0, scalar, in1, op0, op1, R, W):
    S.op("vector", lambda e: e.scalar_tensor_tensor(out=out, in0=in0, scalar=scalar, in1=in1, op0=op0, op1=op1), R, W)


def cp(S, eng, out, in_, R, W):
    if eng == "scalar":
        S.op("scalar", lambda e: e.activation(out=out, in_=in_, func=AF.Copy), R, W)
    else:
        S.op(eng, lambda e: e.tensor_copy(out=out, in_=in_), R, W)


def load_w(S, dst, src, buf, nsplit, axis=1):
    n = dst.shape[axis]
    step = (n + nsplit - 1) // nsplit
    for i in range(0, n, step):
        j = min(n, i + step)
        if axis == 1:
            S.dma("gpsimd", dst[:, i:j], src[:, i:j], parts=[buf])
        else:
            S.dma("gpsimd", dst[:, :, i:j], src[:, :, i:j], parts=[buf])


def stage_mods(g):
    nc, S, L = g.nc, g.S, g.L
    with ExitStack() as st:
        sb = lambda n, s, d=F32: st.enter_context(_sbt(nc, n, list(s), d))
        ct = sb("m_ct", [128, KC, 5])
        cs = sb("m_cs", [128, KC, 5], BF16)
        adb = sb("m_adb", [128, L, 72])
        wbuf = [sb("m_w%d" % i, [128, KC, 1152], BF16) for i in range(2)]
        ps = [st.enter_context(_pst(nc, "m_ps%d" % i, [128, 512], F32)) for i in range(2)]
        b_ct, b_cs, b_adb = Buf("ct"), Buf("cs"), Buf("adb")
        b_w = [Buf("mw0"), Buf("mw1")]
        b_ps = [Buf("mps0"), Buf("mps1")]
        b_mods = Buf("mods")
        S.dma("sync", ct[:], g.cT, writes=[b_ct])
        S.dma("sync", adb[:], g.adab.rearrange("l p j -> p l j"), writes=[b_adb])
        th = sb("m_th", [128, KC, 5])
        act(S, th[:], ct[:], AF.Tanh, [b_ct], [b_cs], scale=0.5)
        stt(S, th[:], th[:], 1.0, ct[:], ALU.add, ALU.mult, [b_ct, b_cs], [b_cs])
        ts(S, "vector", cs[:], th[:], 0.5, None, ALU.mult, None, [b_cs], [b_cs])
        blk = 0
        for l in range(L):
            p = ps[l % 2]
            bp = b_ps[l % 2]
            for nb in range(8):
                w = wbuf[blk % 2]
                bw = b_w[blk % 2]
                for half in range(2):
                    S.dma("gpsimd", w[:, half * 4:(half + 1) * 4, :],
                          g.adaw[l, :, half * 4:(half + 1) * 4, nb * 1152:(nb + 1) * 1152], writes=[bw])
                for jl in range(9):
                    j = nb * 9 + jl
                    for kc in range(KC):
                        mm(S, p[:, j * 5:(j + 1) * 5], w[:, kc, jl * 128:(jl + 1) * 128], cs[:, kc, :],
                           kc == 0, kc == KC - 1, [bw, b_cs], [bp])
                blk += 1
            tt(S, "vector", g.mods[:, l, :, :], p[:, 0:360].rearrange("p (j b) -> p j b", b=5),
               adb[:, l, :].unsqueeze(2).broadcast_to([128, 72, 5]), ALU.add, [bp, b_adb], [b_mods])
            for (ms, vo) in ((1, V_N1), (4, V_NM), (7, V_N2)):
                stt(S, g.mods[:, l, ms * 8:(ms + 1) * 8, :], g.mods[:, l, ms * 8:(ms + 1) * 8, :], 1.0,
                    g.vecs[:, l, vo:vo + 8].unsqueeze(2).broadcast_to([128, 8, 5]), ALU.add, ALU.mult,
                    [b_mods, g.b_const], [b_mods])
            for mg in (2, 5, 8):
                ts(S, "vector", g.mods[:, l, mg * 8:(mg + 1) * 8, :], g.mods[:, l, mg * 8:(mg + 1) * 8, :],
                   0.5, None, ALU.mult, None, [b_mods], [b_mods])
            ts(S, "vector", g.vecs[:, l, V_DW:V_DW + 248], g.vecs[:, l, V_DW:V_DW + 248], 0.5, None,
               ALU.mult, None, [g.b_const, b_mods], [g.b_const])
        S.dma("sync", g.MODS, g.mods[:], reads=[b_mods], sem_buf=b_mods)
        S.barrier()
        S.flush()
        S.release([b_ct, b_cs, b_adb, b_mods] + b_w)


class NormRes:
    pass


def alloc_norm(g, st, pfx):
    nc = g.nc
    sb = lambda n, s, d=F32: st.enter_context(_sbt(nc, pfx + n, list(s), d))
    r = NormRes()
    r.sq = sb("sq", [128, KC, TN], BF16)
    r.u = sb("u", [128, KC, TN])
    r.lr = sb("lr", [128, TN])
    r.rstd = sb("rstd", [128, TN])
    r.n = sb("n", [128, KC, TN], BF16)
    r.pss = st.enter_context(_pst(nc, pfx + "pss", [128, 512], F32))
    r.b_sq, r.b_u, r.b_lr, r.b_rstd, r.b_n, r.b_pss = [Buf(pfx + x) for x in ("sq", "u", "lr", "rstd", "n", "pss")]
    return r


def norm_mod(g, r, h, b_h, l, m_shift, m_A, col):
    S = g.S
    tt(S, "gpsimd", r.sq[:], h[:], h[:], ALU.mult, [b_h], [r.b_sq])
    for c in range(KC):
        mm(S, r.pss[:, 0:TN], g.ones_bf[:], r.sq[:, c, :], c == 0, c == KC - 1, [r.b_sq, g.b_const], [r.b_pss])
    act(S, r.lr[:], r.pss[:, 0:TN], AF.Ln, [r.b_pss, g.b_const], [r.b_lr], scale=1.0 / D, bias=g.cvals[:, 0:1])
    act(S, r.rstd[:], r.lr[:], AF.Exp, [r.b_lr], [r.b_rstd], scale=-0.5)
    tt(S, "vector", r.u[:], h[:], r.rstd[:].unsqueeze(1).broadcast_to([128, KC, TN]), ALU.mult,
       [b_h, r.b_rstd], [r.b_u])
    for c in range(KC):
        act(S, r.n[:, c, :], r.u[:, c, :], AF.Identity, [r.b_u], [r.b_n],
            scale=g.mods[:, l, m_A * 8 + c, col:col + 1], bias=g.mods[:, l, m_shift * 8 + c, col:col + 1])


def tile_list(g, with_ctx=True):
    return [(b, ti) for b in range(g.NBL) for ti in range(NT) if (with_ctx or ti > 0)]


def stage_ffn(g, l, which, with_ctx):
    nc, S = g.nc, g.S
    m_shift, m_A, m_gate = (0, 1, 2) if which == 0 else (6, 7, 8)
    src = g.hT0 if (l == 0 and which == 0) else g.H
    with ExitStack() as st:
        sb = lambda n, s, d=F32: st.enter_context(_sbt(nc, "f_" + n, list(s), d))
        W13 = sb("w13", [128, KC, 2 * DFF], BF16)
        W2 = sb("w2", [128, NJ, D], BF16)
        b_w13, b_w2 = Buf("w13"), Buf("w2")
        load_w(S, W13[:], g.w13[which][l], b_w13, 8, axis=1)
        load_w(S, W2[:], g.w2[which][l], b_w2, 4, axis=1)
        hb = [sb("h%d" % i, [128, KC, TN]) for i in range(2)]
        b_h = [Buf("h0"), Buf("h1")]
        r = alloc_norm(g, st, "f_")
        actb = sb("act", [128, NJ, TN], BF16)
        b_act = Buf("act")
        sa = [sb("sa%d" % i, [128, TN]) for i in range(2)]
        b_sa = [Buf("sa0"), Buf("sa1")]
        psA = [st.enter_context(_pst(nc, "f_psA%d" % i, [128, 512], F32)) for i in range(2)]
        psB = [st.enter_context(_pst(nc, "f_psB%d" % i, [128, 512], F32)) for i in range(2)]
        psO = [st.enter_context(_pst(nc, "f_psO%d" % i, [128, 512], F32)) for i in range(2)]
        b_psA = [Buf("psA0"), Buf("psA1")]
        b_psB = [Buf("psB0"), Buf("psB1")]
        b_psO = [Buf("psO0"), Buf("psO1")]
        tiles = tile_list(g, with_ctx)

        def load(i):
            b, ti = tiles[i]
            S.dma("sync", hb[i % 2][:], src[b, :, :, ti * TN:(ti + 1) * TN], writes=[b_h[i % 2]])

        def norm(i):
            b, ti = tiles[i]
            col = 4 if ti == 0 else b
            norm_mod(g, r, hb[i % 2], b_h[i % 2], l, m_shift, m_A, col)

        load(0)
        norm(0)
        for i, (b, ti) in enumerate(tiles):
            h = hb[i % 2]
            bh = b_h[i % 2]
            col = 4 if ti == 0 else b
            if i + 1 < len(tiles):
                load(i + 1)
            if True:
                for j in range(NJ):
                    pa, pb = psA[j % 2], psB[j % 2]
                    for kc in range(KC):
                        mm(S, pa[:, 0:TN], W13[:, kc, j * 128:(j + 1) * 128], r.n[:, kc, :], kc == 0, kc == KC - 1,
                           [b_w13, r.b_n], [b_psA[j % 2]])
                    for kc in range(KC):
                        mm(S, pb[:, 0:TN], W13[:, kc, DFF + j * 128:DFF + (j + 1) * 128], r.n[:, kc, :], kc == 0,
                           kc == KC - 1, [b_w13, r.b_n], [b_psB[j % 2]])
                    act(S, sa[j % 2][:], pa[:, 0:TN], AF.Silu, [b_psA[j % 2]], [b_sa[j % 2]])
                    tt(S, "vector", actb[:, j, :], sa[j % 2][:], pb[:, 0:TN], ALU.mult, [b_sa[j % 2], b_psB[j % 2]],
                       [b_act])
            if i + 1 < len(tiles):
                norm(i + 1)
            if True:
                for dc in range(KC):
                    po = psO[dc % 2]
                    for j in range(NJ):
                        mm(S, po[:, 0:TN], W2[:, j, dc * 128:(dc + 1) * 128], actb[:, j, :], j == 0, j == NJ - 1,
                           [b_w2, b_act], [b_psO[dc % 2]])
                    stt(S, h[:, dc, :], po[:, 0:TN], g.mods[:, l, m_gate * 8 + dc, col:col + 1], h[:, dc, :],
                        ALU.mult, ALU.add, [b_psO[dc % 2], bh], [bh])
            S.dma("sync", g.H[b, :, :, ti * TN:(ti + 1) * TN], h[:], reads=[bh], sem_buf=bh)
        S.barrier()
        S.flush()
        S.release([b_w13, b_w2] + b_h)


def stage_m1(g, l):
    nc, S = g.nc, g.S
    with ExitStack() as st:
        sb = lambda n, s, d=F32: st.enter_context(_sbt(nc, "a_" + n, list(s), d))
        WQ = sb("wq", [128, KC, 3072], BF16)
        WAB = sb("wab", [128, KC, 32], BF16)
        b_wq, b_wab = Buf("wq"), Buf("wab")
        load_w(S, WQ[:], g.win[l][:, :, O_QKV:O_Z], b_wq, 8, axis=1)
        S.dma("gpsimd", WAB[:], g.win[l][:, :, O_AB:O_GATE], writes=[b_wab])
        hb = [sb("h%d" % i, [128, KC, TN]) for i in range(2)]
        b_h = [Buf("h0"), Buf("h1")]
        r = alloc_norm(g, st, "a_")
        stg = [sb("stg%d" % i, [128, 4, TN]) for i in range(2)]
        b_stg = [Buf("stg0"), Buf("stg1")]
        abst = [sb("abst%d" % i, [128, TN // 128, 32]) for i in range(2)]
        b_abst = [Buf("abst0"), Buf("abst1")]
        ps = [st.enter_context(_pst(nc, "a_ps%d" % i, [128, 512], F32)) for i in range(4)]
        b_ps = [Buf("aps%d" % i) for i in range(4)]
        psab = st.enter_context(_pst(nc, "a_psab", [128, 512], F32))
        b_psab = Buf("psab")
        tiles = tile_list(g)

        def load(i):
            b, ti = tiles[i]
            S.dma("sync", hb[i % 2][:], g.H[b, :, :, ti * TN:(ti + 1) * TN], writes=[b_h[i % 2]])

        load(0)
        for i, (b, ti) in enumerate(tiles):
            h = hb[i % 2]
            col = 4 if ti == 0 else b
            if i + 1 < len(tiles):
                load(i + 1)
            norm_mod(g, r, h, b_h[i % 2], l, 3, 4, col)
            for j in range(24):
                p = ps[j % 4]
                for kc in range(KC):
                    mm(S, p[:, 0:TN], WQ[:, kc, j * 128:(j + 1) * 128], r.n[:, kc, :], kc == 0, kc == KC - 1,
                       [b_wq, r.b_n], [b_ps[j % 4]])
                sg = stg[(j // 4) % 2]
                bsg = b_stg[(j // 4) % 2]
                cp(S, "scalar" if j % 2 == 0 else "vector", sg[:, j % 4, :], p[:, 0:TN], [b_ps[j % 4]], [bsg])
                if j % 4 == 3:
                    S.dma("sync", g.PQ[b, j - 3:j + 1, :, ti * TN:(ti + 1) * TN].rearrange("c p t -> p c t"),
                          sg[:], reads=[bsg], sem_buf=bsg)
            ab = abst[i % 2]
            for tb in range(TN // 128):
                for kc in range(KC):
                    mm(S, psab[:, tb * 32:(tb + 1) * 32], r.n[:, kc, tb * 128:(tb + 1) * 128], WAB[:, kc, :],
                       kc == 0, kc == KC - 1, [b_wab, r.b_n], [b_psab])
            cp(S, "vector", ab[:], psab[:, 0:(TN // 128) * 32].rearrange("p (a f) -> p a f", f=32), [b_psab],
               [b_abst[i % 2]])
            S.dma("sync", g.AB[b, ti * TN:(ti + 1) * TN, :].rearrange("(a p) f -> p a f", p=128), ab[:],
                  reads=[b_abst[i % 2]], sem_buf=b_abst[i % 2])
        S.barrier()
        S.flush()
        S.release([b_wq, b_wab] + b_h + b_stg + b_abst)


def stage_conv(g, l, with_ctx):
    nc, S = g.nc, g.S
    with ExitStack() as st:
        sb = lambda n, s, d=F32: st.enter_context(_sbt(nc, "c_" + n, list(s), d))
        WC = sb("wc", [128, KC, 2048], BF16)
        CP = sb("cp", [128, KC, D], BF16)
        b_wc, b_cp = Buf("wc"), Buf("cpw")
        load_w(S, WC[:], g.win[l][:, :, O_CONV:O_QKV], b_wc, 8, axis=1)
        load_w(S, CP[:], g.cproj[l], b_cp, 4, axis=1)
        hb = [sb("h%d" % i, [128, KC, TN]) for i in range(2)]
        b_h = [Buf("h0"), Buf("h1")]
        r = alloc_norm(g, st, "c_")
        ypL = sb("ypL", [128, KC, 4, 94])
        ypC = sb("ypC", [128, KC, 286])
        b_yp = Buf("yp")
        tg = [sb("tg%d" % i, [128, TN]) for i in range(2)]
        b_tg = [Buf("tg0"), Buf("tg1")]
        cv = sb("cv", [128, KC, TN])
        b_cv = Buf("cv")
        xb = sb("xb", [128, KC, TN], BF16)
        sqb = sb("sqb", [128, KC, TN], BF16)
        b_xb, b_sqb = Buf("xb"), Buf("sqb")
        mean = sb("mean", [128, TN])
        msq = sb("msq", [128, TN])
        var = sb("var", [128, TN])
        rs2 = sb("rs2", [128, TN])
        b_st = Buf("stats")
        sact = sb("sact", [128, KC, TN], BF16)
        b_sact = Buf("sact")
        yst = [sb("yst%d" % i, [128, KC, TN], BF16) for i in range(2)]
        b_yst = [Buf("yst0"), Buf("yst1")]
        psA = [st.enter_context(_pst(nc, "c_psA%d" % i, [128, 512], F32)) for i in range(2)]
        psB = [st.enter_context(_pst(nc, "c_psB%d" % i, [128, 512], F32)) for i in range(2)]
        pst = st.enter_context(_pst(nc, "c_pst", [128, 512], F32))
        psC = [st.enter_context(_pst(nc, "c_psC%d" % i, [128, 512], F32)) for i in range(2)]
        b_psA = [Buf("cpsA0"), Buf("cpsA1")]
        b_psB = [Buf("cpsB0"), Buf("cpsB1")]
        b_pst = Buf("cpst")
        b_psC = [Buf("cpsC0"), Buf("cpsC1")]
        S.op("gpsimd", lambda e: e.memset(ypL[:], 0.0), writes=[b_yp])
        S.op("gpsimd", lambda e: e.memset(ypC[:], 0.0), writes=[b_yp])
        tiles = tile_list(g, with_ctx)

        def load(i):
            b, ti = tiles[i]
            S.dma("sync", hb[i % 2][:], g.H[b, :, :, ti * TN:(ti + 1) * TN], writes=[b_h[i % 2]])

        load(0)
        for i, (b, ti) in enumerate(tiles):
            h = hb[i % 2]
            col = 4 if ti == 0 else b
            ctx = ti == 0
            if i + 1 < len(tiles):
                load(i + 1)
            norm_mod(g, r, h, b_h[i % 2], l, 3, 4, col)
            for c in range(KC):
                pa, pb = psA[c % 2], psB[c % 2]
                for kc in range(KC):
                    mm(S, pa[:, 0:TN], WC[:, kc, c * 128:(c + 1) * 128], r.n[:, kc, :], kc == 0, kc == KC - 1,
                       [b_wc, r.b_n], [b_psA[c % 2]])
                for kc in range(KC):
                    mm(S, pb[:, 0:TN], WC[:, kc, D + c * 128:D + (c + 1) * 128], r.n[:, kc, :], kc == 0, kc == KC - 1,
                       [b_wc, r.b_n], [b_psB[c % 2]])
                act(S, tg[c % 2][:], pb[:, 0:TN], AF.Tanh, [b_psB[c % 2]], [b_tg[c % 2]], scale=0.5)
                if ctx:
                    stt(S, ypC[:, c, 15:15 + TN], tg[c % 2][:], 1.0, pa[:, 0:TN], ALU.add, ALU.mult,
                        [b_tg[c % 2], b_psA[c % 2]], [b_yp])
                else:
                    stt(S, ypL[:, c, :, 15:79], tg[c % 2][:].rearrange("p (r w) -> p r w", w=64), 1.0,
                        pa[:, 0:TN].rearrange("p (r w) -> p r w", w=64), ALU.add, ALU.mult,
                        [b_tg[c % 2], b_psA[c % 2]], [b_yp])
            for c in range(KC):
                if ctx:
                    o_ap = cv[:, c, :]
                    src = lambda k, c=c: ypC[:, c, k:k + TN]
                else:
                    o_ap = cv[:, c, :].rearrange("p (r w) -> p r w", w=64)
                    src = lambda k, c=c: ypL[:, c, :, k:k + 64]
                wv = lambda k, c=c: g.vecs[:, l, V_DW + c * 31 + k:V_DW + c * 31 + k + 1]
                ts(S, "vector", o_ap, src(0), wv(0), g.vecs[:, l, V_CB + c:V_CB + c + 1], ALU.mult, ALU.add,
                   [b_yp, g.b_const], [b_cv])
                for k in range(1, 31):
                    stt(S, o_ap, src(k), wv(k), o_ap, ALU.mult, ALU.add, [b_yp, g.b_const, b_cv], [b_cv])
            cp(S, "gpsimd", xb[:], cv[:], [b_cv], [b_xb])
            tt(S, "gpsimd", sqb[:], cv[:], cv[:], ALU.mult, [b_cv], [b_sqb])
            for c in range(KC):
                mm(S, pst[:, 0:TN], g.ones_bf[:], xb[:, c, :], c == 0, c == KC - 1, [b_xb, g.b_const], [b_pst])
            for c in range(KC):
                mm(S, pst[:, TN:2 * TN], g.ones_bf[:], sqb[:, c, :], c == 0, c == KC - 1, [b_sqb, g.b_const], [b_pst])
            act(S, mean[:], pst[:, 0:TN], AF.Identity, [b_pst], [b_st], scale=1.0 / D)
            tt(S, "gpsimd", msq[:], mean[:], mean[:], ALU.mult, [b_st], [b_st])
            stt(S, var[:], pst[:, TN:2 * TN], 1.0 / D, msq[:], ALU.mult, ALU.subtract, [b_pst, b_st], [b_st])
            act(S, var[:], var[:], AF.Ln, [b_st, g.b_const], [b_st], bias=g.cvals[:, 0:1])
            act(S, rs2[:], var[:], AF.Exp, [b_st], [b_st], scale=-0.5)
            tt(S, "vector", cv[:], cv[:], mean[:].unsqueeze(1).broadcast_to([128, KC, TN]), ALU.subtract,
               [b_cv, b_st], [b_cv])
            tt(S, "vector", cv[:], cv[:], rs2[:].unsqueeze(1).broadcast_to([128, KC, TN]), ALU.mult,
               [b_cv, b_st], [b_cv])
            for c in range(KC):
                act(S, sact[:, c, :], cv[:, c, :], AF.Silu, [b_cv, g.b_const], [b_sact],
                    scale=g.vecs[:, l, V_LG + c:V_LG + c + 1], bias=g.vecs[:, l, V_LB + c:V_LB + c + 1])
            ys = yst[i % 2]
            for dc in range(KC):
                pc = psC[dc % 2]
                for c in range(KC):
                    mm(S, pc[:, 0:TN], CP[:, c, dc * 128:(dc + 1) * 128], sact[:, c, :], c == 0, c == KC - 1,
                       [b_cp, b_sact], [b_psC[dc % 2]])
                cp(S, "scalar" if dc % 2 == 0 else "vector", ys[:, dc, :], pc[:, 0:TN], [b_psC[dc % 2]], [b_yst[i % 2]])
            S.dma("sync", g.YC[b, :, :, ti * TN:(ti + 1) * TN], ys[:], reads=[b_yst[i % 2]], sem_buf=b_yst[i % 2])
        S.barrier()
        S.flush()
        S.release([b_wc, b_cp] + b_h + b_yst)


HG = 2
NHD = 2 * HG


def stage_dn(g, l):
    nc, S = g.nc, g.S
    with ExitStack() as st:
        sb = lambda n, s, d=F32: st.enter_context(_sbt(nc, "d_" + n, list(s), d))
        B = lambda n: Buf("d_" + n)
        raw = sb("raw", [128, TP]); b_raw = B("raw")
        y = sb("y", [128, TP]); b_y = B("y")
        t = sb("t", [128, TP]); b_t = B("t")
        sqb = sb("sqb", [128, TP], BF16); b_sqb = B("sqb")
        qT = sb("qT", [128, HG, TP], BF16); b_qT = B("qT")
        kT = sb("kT", [128, HG, TP], BF16); b_kT = B("kT")
        vT = sb("vT", [128, HG, TP], BF16); b_vT = B("vT")
        ktok = sb("ktok", [64, NCH, HG, 128], BF16); b_ktok = B("ktok")
        vtok = sb("vtok", [64, NCH, HG, 128], BF16); b_vtok = B("vtok")
        oacc = sb("oacc", [64, NCH, HG, 128]); b_oacc = B("oacc")
        abraw = sb("abraw", [64, NCH, 32]); b_abraw = B("abraw")
        gg = sb("gg", [64, NCH, 16]); b_gg = B("gg")
        ghi = sb("ghi", [64, NCH, 16], BF16)
        glo = sb("glo", [64, NCH, 16], BF16)
        cmb = sb("cmb", [64, 10, 64], BF16)
        ones64 = g.ones_bf[0:64, :]
        beta = sb("beta", [64, NCH, 16]); b_beta = B("beta")
        tmpa = sb("tmpa", [64, NCH, 16]); b_tmpa = B("tmpa")
        nA = sb("nA", [64, 16]); b_nA = B("nA")
        Sst = sb("S", [128, NHD, 128]); b_S = [B("S%d" % i) for i in range(NHD)]
        Sbf = sb("Sbf", [128, NHD, 128], BF16); b_Sbf = [B("Sbf%d" % i) for i in range(NHD)]
        ssq = sb("ossq", [64, NCH]); b_ssq = B("ossq")
        R = 2
        ring = []
        for ri in range(R):
            q = Ctx()
            q.gtri = sb("gtri%d" % ri, [64, 2, NHD, 64], BF16); q.b_gtri = B("gtri%d" % ri)
            q.Gsb = sb("Gsb%d" % ri, [128, NHD, 64]); q.b_Gsb = B("Gsb%d" % ri)
            q.EGbc = sb("EGbc%d" % ri, [128, NHD, 64]); q.b_EGbc = B("EGbc%d" % ri)
            q.Gc = sb("Gc%d" % ri, [64, NHD]); q.b_Gc = B("Gc%d" % ri)
            q.nEGc = sb("nEGc%d" % ri, [64, NHD]); q.b_nEGc = B("nEGc%d" % ri)
            q.Dm = sb("Dm%d" % ri, [64, NHD, 64]); q.b_Dm = B("Dm%d" % ri)
            q.decT = sb("decT%d" % ri, [64, NHD, 64]); q.b_decT = B("decT%d" % ri)
            q.Xf = sb("Xf%d" % ri, [64, NHD, 64], BF16); q.b_Xf = B("Xf%d" % ri)
            q.Xo = [sb("Xo%d_%d" % (ri, k), [64, NHD, 64], BF16) for k in range(3)]
            q.b_Xo = B("Xo%d" % ri)
            q.Tn = [sb("Tn%d_%d" % (ri, k), [64, NHD, 64], BF16) for k in range(2)]
            q.b_Tn = [B("Tn%d_%d" % (ri, k)) for k in range(2)]
            q.TTn = [sb("TTn%d_%d" % (ri, k), [64, NHD, 64], BF16) for k in range(2)]
            q.b_TTn = [B("TTn%d_%d" % (ri, k)) for k in range(2)]
            q.Mm = sb("Mm%d" % ri, [64, NHD, 64], BF16); q.b_Mm = B("Mm%d" % ri)
            q.X = [sb("X%d_%d" % (ri, k), [64, NHD, 64], BF16) for k in range(2)]
            q.b_X = [B("X%d_%d" % (ri, k)) for k in range(2)]
            q.XT = [sb("XT%d_%d" % (ri, k), [64, NHD, 64], BF16) for k in range(2)]
            q.b_XT = [B("XT%d_%d" % (ri, k)) for k in range(2)]
            q.P = [sb("P%d_%d" % (ri, k), [64, NHD, 64], BF16) for k in range(2)]
            q.b_P = [B("P%d_%d" % (ri, k)) for k in range(2)]
            q.MT = sb("MT%d" % ri, [64, NHD, 64], BF16); q.b_MT = B("MT%d" % ri)
            q.qg = sb("qg%d" % ri, [128, NHD, 64], BF16); q.b_qg = B("qg%d" % ri)
            q.e2 = sb("e2%d" % ri, [64, NHD]); q.b_e2 = B("e2%d" % ri)
            q.kd = sb("kd%d" % ri, [64, NHD, 128], BF16); q.b_kd = B("kd%d" % ri)
            q.eGl = sb("eGl%d" % ri, [128, NHD]); q.b_eGl = B("eGl%d" % ri)
            ring.append(q)
        r0 = sb("r0", [64, NHD, 128], BF16); b_r0 = [B("r0_%d" % i) for i in range(NHD)]
        vnew = sb("vnew", [64, NHD, 128], BF16); b_vnew = [B("vnew%d" % i) for i in range(NHD)]
        psG = st.enter_context(_pst(nc, "d_psG", [128, 512], F32)); b_psG = B("psG")
        psK = st.enter_context(_pst(nc, "d_psK", [128, 512], F32)); b_psK = B("psK")
        psN = st.enter_context(_pst(nc, "d_psN", [128, 512], F32)); b_psN = B("psN")
        psP = st.enter_context(_pst(nc, "d_psP", [128, 512], F32)); b_psP = B("psP")
        psT = st.enter_context(_pst(nc, "d_psT", [128, 512], F32)); b_psT = B("psT")
        psKS = st.enter_context(_pst(nc, "d_psKS", [128, 512], F32)); b_psKS = B("psKS")
        psVN = st.enter_context(_pst(nc, "d_psVN", [128, 512], F32)); b_psVN = B("psVN")
        psO = st.enter_context(_pst(nc, "d_psO", [128, 512], F32)); b_psO = B("psO")
        scan_ps = [(psKS, b_psKS), (psVN, b_psVN), (psO, b_psO), (psP, b_psP)]
        cm, cb = g.cm, g.b_const
        id64 = g.ident[0:64, 0:64]

        S.op("gpsimd", lambda e: e.memset(raw[:], 0.0), writes=[b_raw])
        cp(S, "gpsimd", cmb[:], cm[:], [cb], [b_nA])
        act(S, nA[:], g.dnc[:, l, 0, :], AF.Exp, [cb], [b_nA])
        ts(S, "gpsimd", nA[:], nA[:], -1.0, None, ALU.mult, None, [b_nA], [b_nA])

        def chunk_of(d, s):
            if d == 0:
                return s
            return 3 - s if s < 4 else 39 - s

        for b in range(g.NBL):
            if g.dn_phase == "0":
                break
            S.dma("sync", abraw[:], g.AB[b].rearrange("(c p) f -> p c f", p=64), writes=[b_abraw])
            tt(S, "vector", tmpa[:], abraw[:, :, 0:16], g.dnc[:, l, 1, :].unsqueeze(1).broadcast_to([64, NCH, 16]),
               ALU.add, [b_abraw, cb], [b_tmpa])
            ts(S, "vector", tmpa[:], tmpa[:], 30.0, None, ALU.min, None, [b_tmpa], [b_tmpa])
            act(S, tmpa[:], tmpa[:], AF.Exp, [b_tmpa], [b_tmpa])
            act(S, tmpa[:], tmpa[:], AF.Ln, [b_tmpa, cb], [b_tmpa], bias=g.cvals[0:64, 2:3])
            tt(S, "vector", gg[:], tmpa[:], nA[:].unsqueeze(1).broadcast_to([64, NCH, 16]), ALU.mult,
               [b_tmpa, b_nA], [b_gg])
            cp(S, "gpsimd", ghi[:], gg[:], [b_gg], [b_gg])
            tt(S, "gpsimd", glo[:], gg[:], ghi[:], ALU.subtract, [b_gg], [b_gg])
            act(S, beta[:], abraw[:, :, 16:32], AF.Exp, [b_abraw], [b_beta], scale=-1.0)
            ts(S, "vector", beta[:], beta[:], 1.0, None, ALU.add, None, [b_beta], [b_beta])
            S.op("vector", lambda e: e.reciprocal(out=beta[:], in_=beta[:]), [b_beta], [b_beta])
            for hg in range(8 // HG):
                h0 = hg * HG
                for sel in range(3):
                    dst, b_dst = ((qT, b_qT), (kT, b_kT), (vT, b_vT))[sel]
                    for hl in range(HG):
                        jj = sel * 8 + h0 + hl
                        S.dma("sync", raw[:, 1:1 + CTXL], g.PQ[b, jj, :, 0:CTXL], parts=[b_raw])
                        S.dma("sync", raw[:, 2 + CTXL:2 + T], g.PQ[b, jj, :, CTXL:T], parts=[b_raw])
                        w = lambda k: g.vecs[:, l, V_SH + jj * 3 + k:V_SH + jj * 3 + k + 1]
                        ts(S, "vector", y[:, 1:TP - 1], raw[:, 1:TP - 1], w(1), None, ALU.mult, None, [b_raw, cb], [b_y])
                        stt(S, y[:, 1:TP - 1], raw[:, 0:TP - 2], w(0), y[:, 1:TP - 1], ALU.mult, ALU.add, [b_raw, cb, b_y], [b_y])
                        stt(S, y[:, 1:TP - 1], raw[:, 2:TP], w(2), y[:, 1:TP - 1], ALU.mult, ALU.add, [b_raw, cb, b_y], [b_y])
                        act(S, t[:, 1:TP - 1], y[:, 1:TP - 1], AF.Tanh, [b_y], [b_t], scale=0.5)
                        stt(S, t[:, 1:TP - 1], t[:, 1:TP - 1], 1.0, y[:, 1:TP - 1], ALU.add, ALU.mult, [b_y, b_t], [b_t])
                        if sel == 2:
                            ts(S, "gpsimd", vT[:, hl, 1:TP - 1], t[:, 1:TP - 1], 0.5, None, ALU.mult, None, [b_t], [b_vT])
                            continue
                        tt(S, "gpsimd", sqb[:, 1:TP - 1], t[:, 1:TP - 1], t[:, 1:TP - 1], ALU.mult, [b_t], [b_sqb])
                        pos = 1
                        pi = 0
                        while pos < TP - 1:
                            n = min(512, TP - 1 - pos)
                            ps_, bps_ = scan_ps[pi % 4]
                            mm(S, ps_[:, 0:n], g.ones_bf[:], sqb[:, pos:pos + n], True, True, [b_sqb, cb], [bps_])
                            act(S, y[:, pos:pos + n], ps_[:, 0:n], AF.Ln, [bps_, cb], [b_y], bias=g.cvals[:, 1:2])
                            pos += n
                            pi += 1
                        act(S, y[:, 1:TP - 1], y[:, 1:TP - 1], AF.Exp, [b_y], [b_y], scale=-0.5)
                        stt(S, dst[:, hl, 1:TP - 1], t[:, 1:TP - 1], (128.0 ** -0.5) if sel == 0 else 1.0, y[:, 1:TP - 1],
                            ALU.mult, ALU.mult, [b_t, b_y], [b_dst])
                for (srcT, b_src, dstk, b_dk) in ((kT, b_kT, ktok, b_ktok), (vT, b_vT, vtok, b_vtok)):
                    for hl in range(HG):
                        for c0 in range(0, NCH, 4):
                            ncg = min(4, NCH - c0)
                            for ci in range(ncg):
                                c = c0 + ci
                                tr(S, psT[0:64, ci * 128:(ci + 1) * 128], srcT[:, hl, ptok(c):ptok(c) + 64], g.ident[:],
                                   [b_src, cb], [b_psT])
                            cp(S, "scalar" if (c0 // 4) % 2 == 0 else "vector", dstk[:, c0:c0 + ncg, hl, :],
                               psT[0:64, 0:ncg * 128].rearrange("p (c f) -> p c f", f=128), [b_psT], [b_dk])
                S.op("gpsimd", lambda e: e.memset(oacc[:], 0.0), writes=[b_oacc])
                for i in range(NHD):
                    S.op("gpsimd", lambda e, i=i: e.memset(Sst[:, i, :], 0.0), writes=[b_S[i]])
                    S.op("gpsimd", lambda e, i=i: e.memset(Sbf[:, i, :], 0.0), writes=[b_Sbf[i]])

                def B_pre(s):
                    q = ring[s % R]
                    cs_ = [chunk_of(0, s), chunk_of(1, s)]
                    for d in range(2):
                        for hi_, gsrc in enumerate((ghi, glo)):
                            gsl = gsrc[:, cs_[d], d * 8 + h0:d * 8 + h0 + HG]
                            tt(S, "gpsimd", q.gtri[:, hi_, d * HG:(d + 1) * HG, :], gsl.unsqueeze(2).broadcast_to([64, HG, 64]),
                               cmb[:, d, :].unsqueeze(1).broadcast_to([64, HG, 64]), ALU.mult, [b_gg, b_nA], [q.b_gtri])
                    for hi_ in range(2):
                        mm(S, psG[:, 0:NHD * 64], ones64, q.gtri[:, hi_].rearrange("p a b -> p (a b)"), hi_ == 0, hi_ == 1,
                           [q.b_gtri, cb], [b_psG])
                    for d in range(2):
                        for hi_, gsrc in enumerate((ghi, glo)):
                            mm(S, psG[0:64, 256 + d * HG:256 + (d + 1) * HG], cmb[:, d, :],
                               gsrc[:, cs_[d], d * 8 + h0:d * 8 + h0 + HG], hi_ == 0, hi_ == 1, [b_gg, b_nA], [b_psG])
                    cp(S, "scalar", q.Gsb[:].rearrange("p a b -> p (a b)"), psG[:, 0:NHD * 64], [b_psG], [q.b_Gsb])
                    act(S, q.EGbc[:].rearrange("p a b -> p (a b)"), psG[:, 0:NHD * 64], AF.Exp, [b_psG], [q.b_EGbc])
                    cp(S, "scalar", q.Gc[:], psG[0:64, 256:256 + NHD], [b_psG], [q.b_Gc])
                    act(S, q.nEGc[:], psG[0:64, 256:256 + NHD], AF.Exp, [b_psG], [q.b_nEGc])
                    ts(S, "gpsimd", q.nEGc[:], q.nEGc[:], -1.0, None, ALU.mult, None, [q.b_nEGc], [q.b_nEGc])
                    tt(S, "vector", q.Dm[:], q.Gsb[0:64], q.Gc[:].unsqueeze(2).broadcast_to([64, NHD, 64]), ALU.subtract,
                       [q.b_Gsb, q.b_Gc], [q.b_Dm])
                    for d in range(2):
                        stt(S, q.Dm[:, d * HG:(d + 1) * HG, :], q.Dm[:, d * HG:(d + 1) * HG, :], 0.0,
                            cm[:, 2 + d, :].unsqueeze(1).broadcast_to([64, HG, 64]), ALU.min, ALU.add,
                            [q.b_Dm, cb], [q.b_Dm])
                    act(S, q.decT[:], q.Dm[:], AF.Exp, [q.b_Dm], [q.b_decT])
                    for d in range(2):
                        for hl in range(HG):
                            dh = d * HG + hl
                            ksl = kT[:, hl, ptok(cs_[d]):ptok(cs_[d]) + 64]
                            mm(S, psK[0:64, dh * 64:(dh + 1) * 64], ksl, ksl, True, True, [b_kT], [b_psK])
                            mm(S, psK[0:64, 256 + dh * 64:256 + (dh + 1) * 64], ksl,
                               qT[:, hl, ptok(cs_[d]):ptok(cs_[d]) + 64], True, True, [b_kT, b_qT], [b_psK])
                    for d in range(2):
                        for hl in range(HG):
                            dh = d * HG + hl
                            stt(S, q.Xf[:, dh, :], psK[0:64, dh * 64:(dh + 1) * 64],
                                beta[:, cs_[d], d * 8 + h0 + hl:d * 8 + h0 + hl + 1], q.decT[:, dh, :], ALU.mult, ALU.mult,
                                [b_psK, b_beta, q.b_decT], [q.b_Xf])
                    tt(S, "vector", q.MT[:], psK[0:64, 256:512].rearrange("p (a b) -> p a b", b=64), q.decT[:], ALU.mult,
                       [b_psK, q.b_decT], [q.b_MT])

                NLEV = 2

                def B_neu(s, k):
                    q = ring[s % R]
                    mk = lambda m: cmb[:, m, :].unsqueeze(1).broadcast_to([64, NHD, 64])
                    if k == 0:
                        tt(S, "gpsimd", q.X[0][:], q.Xf[:], mk(6), ALU.mult, [q.b_Xf, b_nA], [q.b_X[0]])
                        for lv in range(3):
                            tt(S, "gpsimd", q.Xo[lv][:], q.Xf[:], mk(7 + lv), ALU.mult, [q.b_Xf, b_nA], [q.b_Xo])
                        for dh in range(NHD):
                            tr(S, psT[0:64, dh * 64:(dh + 1) * 64], q.X[0][:, dh, :], id64, [q.b_X[0], cb], [b_psT])
                        cp(S, "scalar", q.XT[0][:].rearrange("p a b -> p (a b)"), psT[0:64, 0:NHD * 64], [b_psT], [q.b_XT[0]])
                        stt(S, q.P[0][:], q.X[0][:], -1.0, id64.unsqueeze(1).broadcast_to([64, NHD, 64]), ALU.mult, ALU.add,
                            [q.b_X[0], cb], [q.b_P[0]])
                        return
                    if k <= NLEV + 1:
                        pv, cu = (k - 1) % 2, k % 2
                        if k <= NLEV:
                            for dh in range(NHD):
                                if k < NLEV:
                                    mm(S, psN[0:64, dh * 64:(dh + 1) * 64], q.XT[pv][:, dh, :], q.X[pv][:, dh, :], True, True,
                                       [q.b_XT[pv], q.b_X[pv]], [b_psN])
                                mm(S, psN[0:64, 256 + dh * 64:256 + (dh + 1) * 64], q.X[pv][:, dh, :], q.XT[pv][:, dh, :], True, True,
                                   [q.b_XT[pv], q.b_X[pv]], [b_psN])
                        if k >= 2:
                            for dh in range(NHD):
                                mm(S, psP[0:64, dh * 64:(dh + 1) * 64], q.XT[pv][:, dh, :], q.P[k % 2][:, dh, :], True, True,
                                   [q.b_XT[pv], q.b_P[k % 2]], [b_psP])
                        if k <= NLEV:
                            if k < NLEV:
                                cp(S, "scalar", q.X[cu][:].rearrange("p a b -> p (a b)"), psN[0:64, 0:256], [b_psN], [q.b_X[cu]])
                            cp(S, "scalar", q.XT[cu][:].rearrange("p a b -> p (a b)"), psN[0:64, 256:512], [b_psN], [q.b_XT[cu]])
                        if k >= 2:
                            tt(S, "vector", q.P[(k + 1) % 2][:].rearrange("p a b -> p (a b)"),
                               q.P[k % 2][:].rearrange("p a b -> p (a b)"), psP[0:64, 0:256], ALU.add,
                               [q.b_P[k % 2], b_psP], [q.b_P[(k + 1) % 2]])
                        return
                    fin = (NLEV + 2) % 2
                    if k == NLEV + 2:
                        for dh in range(NHD):
                            tr(S, psT[0:64, dh * 64:(dh + 1) * 64], q.P[fin][:, dh, :], id64, [q.b_P[fin], cb], [b_psT])
                        cp(S, "scalar", q.Tn[0][:].rearrange("p a b -> p (a b)"), psT[0:64, 0:NHD * 64], [b_psT], [q.b_Tn[0]])
                        return
                    kk_ = k - (NLEV + 3)
                    lv, ph = kk_ // 2, kk_ % 2
                    Tc, b_Tc = q.Tn[lv % 2], q.b_Tn[lv % 2]
                    Tx, b_Tx = q.Tn[(lv + 1) % 2], q.b_Tn[(lv + 1) % 2]
                    if lv == 0:
                        TTc, b_TTc = q.P[fin], q.b_P[fin]
                    else:
                        TTc, b_TTc = q.TTn[lv % 2], q.b_TTn[lv % 2]
                    TTx, b_TTx = q.TTn[(lv + 1) % 2], q.b_TTn[(lv + 1) % 2]
                    if ph == 0:
                        for dh in range(NHD):
                            mm(S, psN[0:64, dh * 64:(dh + 1) * 64], q.Xo[lv][:, dh, :], Tc[:, dh, :], True, True,
                               [q.b_Xo, b_Tc], [b_psN])
                        cp(S, "scalar", q.Mm[:].rearrange("p a b -> p (a b)"), psN[0:64, 0:256], [b_psN], [q.b_Mm])
                    else:
                        for dh in range(NHD):
                            if lv < 2:
                                mm(S, psP[0:64, dh * 64:(dh + 1) * 64], TTc[:, dh, :], q.Mm[:, dh, :], True, True,
                                   [b_TTc, q.b_Mm], [b_psP])
                            mm(S, psP[0:64, 256 + dh * 64:256 + (dh + 1) * 64], q.Mm[:, dh, :], TTc[:, dh, :], True, True,
                               [b_TTc, q.b_Mm], [b_psP])
                        if lv < 2:
                            tt(S, "vector", Tx[:].rearrange("p a b -> p (a b)"), Tc[:].rearrange("p a b -> p (a b)"),
                               psP[0:64, 0:256], ALU.subtract, [b_Tc, b_psP], [b_Tx])
                        tt(S, "vector", TTx[:].rearrange("p a b -> p (a b)"), TTc[:].rearrange("p a b -> p (a b)"),
                           psP[0:64, 256:512], ALU.subtract, [b_TTc, b_psP], [b_TTx])

                NB_ = NLEV + 3 + 6

                def B_post(s):
                    q = ring[s % R]
                    cs_ = [chunk_of(0, s), chunk_of(1, s)]
                    for d in range(2):
                        last = 63 if d == 0 else 0
                        sl = slice(d * HG, (d + 1) * HG)
                        tt(S, "gpsimd", q.qg[:, sl, :], qT[:, :, ptok(cs_[d]):ptok(cs_[d]) + 64], q.EGbc[:, sl, :], ALU.mult,
                           [b_qT, q.b_EGbc], [q.b_qg])
                        tt(S, "gpsimd", q.e2[:, sl], q.Gsb[0:64, sl, last], q.Gc[:, sl], ALU.subtract, [q.b_Gsb, q.b_Gc], [q.b_e2])
                        cp(S, "gpsimd", q.eGl[:, sl], q.EGbc[:, sl, last], [q.b_EGbc], [q.b_eGl])
                    act(S, q.e2[:], q.e2[:], AF.Exp, [q.b_e2], [q.b_e2])
                    for d in range(2):
                        sl = slice(d * HG, (d + 1) * HG)
                        tt(S, "gpsimd", q.kd[:, sl, :], ktok[:, cs_[d], :, :], q.e2[:, sl].unsqueeze(2).broadcast_to([64, HG, 128]),
                           ALU.mult, [b_ktok, q.b_e2], [q.b_kd])

                def C_step(s, part):
                    q = ring[s % R]
                    TTm, b_TT = q.TTn[1], q.b_TTn[1]
                    for d in range(2):
                        c = chunk_of(d, s)
                        for hl in range(HG):
                            dh = d * HG + hl
                            bcol = d * 8 + h0 + hl
                            ksl = kT[:, hl, ptok(c):ptok(c) + 64]
                            if part == 0:
                                mm(S, psKS[0:64, dh * 128:(dh + 1) * 128], ksl, Sbf[:, dh, :], True, True,
                                   [b_kT, b_Sbf[dh]], [b_psKS])
                                stt(S, r0[:, dh, :], psKS[0:64, dh * 128:(dh + 1) * 128], q.nEGc[:, dh:dh + 1],
                                    vtok[:, c, hl, :], ALU.mult, ALU.add, [b_psKS, q.b_nEGc, b_vtok], [b_r0[dh]])
                            elif part == 1:
                                mm(S, psVN[0:64, dh * 128:(dh + 1) * 128], TTm[:, dh, :], r0[:, dh, :], True, True,
                                   [b_TT, b_r0[dh]], [b_psVN])
                                act(S, vnew[:, dh, :], psVN[0:64, dh * 128:(dh + 1) * 128], AF.Identity,
                                    [b_psVN, b_beta], [b_vnew[dh]], scale=beta[:, c, bcol:bcol + 1])
                            elif part == 2:
                                mm(S, psO[0:64, dh * 128:(dh + 1) * 128], q.MT[:, dh, :], vnew[:, dh, :], True, False,
                                   [q.b_MT, b_vnew[dh]], [b_psO])
                                mm(S, psO[0:64, dh * 128:(dh + 1) * 128], q.qg[:, dh, :], Sbf[:, dh, :], False, True,
                                   [q.b_qg, b_Sbf[dh]], [b_psO])
                                tt(S, "vector", oacc[:, c, hl, :], oacc[:, c, hl, :], psO[0:64, dh * 128:(dh + 1) * 128], ALU.add,
                                   [b_psO, b_oacc], [b_oacc])
                            else:
                                mm(S, psKS[:, dh * 128:(dh + 1) * 128], q.kd[:, dh, :], vnew[:, dh, :], True, True,
                                   [q.b_kd, b_vnew[dh]], [b_psKS])
                                stt(S, Sst[:, dh, :], Sst[:, dh, :], q.eGl[:, dh:dh + 1], psKS[:, dh * 128:(dh + 1) * 128],
                                    ALU.mult, ALU.add, [b_psKS, q.b_eGl, b_S[dh]], [b_S[dh]])
                                cp(S, "gpsimd", Sbf[:, dh, :], Sst[:, dh, :], [b_S[dh]], [b_Sbf[dh]])

                if g.dn_phase == "A":
                    S.dma("sync", g.OT[b, :, h0, :], kT[:, 0, 1:1 + T], reads=[b_kT], sem_buf=b_kT)
                    S.dma("sync", g.OT[b, :, h0 + 1, :], vT[:, 0, 1:1 + T], reads=[b_vT], sem_buf=b_vT)
                    break
                B_pre(0)
                for k in range(NB_):
                    B_neu(0, k)
                B_post(0)
                sched = [(0, 1), (1, 3), (2, 6), (3, 9)]
                for s in range(NCH if g.dn_phase is None else int(g.dn_phase)):
                    nxt = s + 1 < NCH
                    if nxt:
                        B_pre(s + 1)
                    kb = 0
                    for part, upto in sched:
                        if nxt:
                            while kb < upto:
                                B_neu(s + 1, kb)
                                kb += 1
                        C_step(s, part)
                    if nxt:
                        while kb < NB_:
                            B_neu(s + 1, kb)
                            kb += 1
                        B_post(s + 1)

                if g.dn_phase is not None and g.dn_phase != "36":
                    break
                otmp = vT[0:64].rearrange("p a b -> p (a b)")[:, 0:NCH * 128].rearrange("p (c f) -> p c f", f=128)
                oT = sqb
                for hl in range(HG):
                    tt(S, "gpsimd", otmp, oacc[:, :, hl, :], oacc[:, :, hl, :], ALU.mult, [b_oacc], [b_vT])
                    S.op("vector", lambda e: e.reduce_sum(out=ssq[:], in_=otmp, axis=AX.X), [b_vT], [b_ssq])
                    act(S, ssq[:], ssq[:], AF.Ln, [b_ssq, cb], [b_ssq], scale=1.0 / 128.0, bias=g.cvals[0:64, 0:1])
                    act(S, ssq[:], ssq[:], AF.Exp, [b_ssq], [b_ssq], scale=-0.5)
                    tt(S, "vector", otmp, oacc[:, :, hl, :], ssq[:].unsqueeze(2).broadcast_to([64, NCH, 128]), ALU.mult,
                       [b_oacc, b_ssq], [b_vT])
                    for c0 in range(0, NCH, 8):
                        ncg = min(8, NCH - c0)
                        for ci in range(ncg):
                            tr(S, psT[:, ci * 64:(ci + 1) * 64], otmp[:, c0 + ci, :], id64, [b_vT, cb], [b_psT])
                        cp(S, "scalar" if (c0 // 8) % 2 == 0 else "vector", oT[:, c0 * 64:(c0 + ncg) * 64], psT[:, 0:ncg * 64],
                           [b_psT], [b_sqb])
                    S.dma("sync", g.OT[b, :, h0 + hl, :], oT[:, 0:T], reads=[b_sqb], sem_buf=b_sqb)
        S.barrier()
        S.flush()
        S.release([b_raw, b_abraw, b_sqb])


def stage_m3(g, l, with_ctx):
    nc, S = g.nc, g.S
    with ExitStack() as st:
        sb = lambda n, s, d=F32: st.enter_context(_sbt(nc, "e_" + n, list(s), d))
        WZ = sb("wz", [128, KC, D], BF16)
        WG = sb("wg", [128, KC, 2 * D], BF16)
        DP = sb("dp", [128, KC, D], BF16)
        WO = sb("wo", [128, KC, D], BF16)
        b_wz, b_wg, b_dp, b_wo = Buf("wz"), Buf("wg"), Buf("dp"), Buf("wo")
        load_w(S, WZ[:], g.win[l][:, :, O_Z:O_AB], b_wz, 4, axis=1)
        load_w(S, WG[:], g.win[l][:, :, O_GATE:NIN], b_wg, 8, axis=1)
        load_w(S, DP[:], g.dproj[l], b_dp, 4, axis=1)
        load_w(S, WO[:], g.wout[l], b_wo, 4, axis=1)
        hb = [sb("h%d" % i, [128, KC, TN]) for i in range(2)]
        b_h = [Buf("h0"), Buf("h1")]
        otb = [sb("ot%d" % i, [128, KC, TN], BF16) for i in range(2)]
        b_ot = [Buf("ot0"), Buf("ot1")]
        ycb = [sb("yc%d" % i, [128, KC, TN], BF16) for i in range(2)]
        b_yc = [Buf("yc0"), Buf("yc1")]
        r = alloc_norm(g, st, "e_")
        sz = [sb("sz%d" % i, [128, TN]) for i in range(2)]
        b_sz = [Buf("sz0"), Buf("sz1")]
        og = sb("og", [128, KC, TN], BF16)
        b_og = Buf("og")
        tA = [sb("tA%d" % i, [128, TN]) for i in range(2)]
        tB = [sb("tB%d" % i, [128, TN]) for i in range(2)]
        mA = [sb("mA%d" % i, [128, TN]) for i in range(2)]
        mB = [sb("mB%d" % i, [128, TN]) for i in range(2)]
        b_tA = [Buf("tA0"), Buf("tA1")]
        b_tB = [Buf("tB0"), Buf("tB1")]
        b_mA = [Buf("mA0"), Buf("mA1")]
        b_mB = [Buf("mB0"), Buf("mB1")]
        mt = sb("m", [128, KC, TN], BF16)
        b_m = Buf("m")
        psZ = [st.enter_context(_pst(nc, "e_psZ%d" % i, [128, 512], F32)) for i in range(2)]
        psD = st.enter_context(_pst(nc, "e_psD", [128, 512], F32))
        psA = st.enter_context(_pst(nc, "e_psA", [128, 512], F32))
        psB = st.enter_context(_pst(nc, "e_psB", [128, 512], F32))
        psO = [st.enter_context(_pst(nc, "e_psO%d" % i, [128, 512], F32)) for i in range(2)]
        b_psZ = [Buf("epsZ0"), Buf("epsZ1")]
        b_psD, b_psA, b_psB = Buf("epsD"), Buf("epsA"), Buf("epsB")
        b_psO = [Buf("epsO0"), Buf("epsO1")]
        tiles = tile_list(g, with_ctx)

        def load(i):
            b, ti = tiles[i]
            sl = slice(ti * TN, (ti + 1) * TN)
            S.dma("sync", hb[i % 2][:], g.H[b, :, :, sl], writes=[b_h[i % 2]])
            S.dma("sync", otb[i % 2][:], g.OT[b, :, :, sl], writes=[b_ot[i % 2]])
            S.dma("sync", ycb[i % 2][:], g.YC[b, :, :, sl], writes=[b_yc[i % 2]])

        load(0)
        for i, (b, ti) in enumerate(tiles):
            h, ot, yc = hb[i % 2], otb[i % 2], ycb[i % 2]
            bh, bot, byc = b_h[i % 2], b_ot[i % 2], b_yc[i % 2]
            col = 4 if ti == 0 else b
            if i + 1 < len(tiles):
                load(i + 1)
            norm_mod(g, r, h, bh, l, 3, 4, col)
            for hh in range(KC):
                pz = psZ[hh % 2]
                for kc in range(KC):
                    mm(S, pz[:, 0:TN], WZ[:, kc, hh * 128:(hh + 1) * 128], r.n[:, kc, :], kc == 0, kc == KC - 1,
                       [b_wz, r.b_n], [b_psZ[hh % 2]])
                act(S, sz[hh % 2][:], pz[:, 0:TN], AF.Silu, [b_psZ[hh % 2]], [b_sz[hh % 2]])
                stt(S, og[:, hh, :], sz[hh % 2][:], g.vecs[:, l, V_ON:V_ON + 1], ot[:, hh, :], ALU.mult, ALU.mult,
                    [b_sz[hh % 2], bot, g.b_const], [b_og])
            for dc in range(KC):
                for hh in range(KC):
                    mm(S, psD[:, 0:TN], DP[:, hh, dc * 128:(dc + 1) * 128], og[:, hh, :], hh == 0, hh == KC - 1,
                       [b_dp, b_og], [b_psD])
                for kc in range(KC):
                    mm(S, psA[:, 0:TN], WG[:, kc, dc * 128:(dc + 1) * 128], r.n[:, kc, :], kc == 0, kc == KC - 1,
                       [b_wg, r.b_n], [b_psA])
                for kc in range(KC):
                    mm(S, psB[:, 0:TN], WG[:, kc, D + dc * 128:D + (dc + 1) * 128], r.n[:, kc, :], kc == 0, kc == KC - 1,
                       [b_wg, r.b_n], [b_psB])
                p = dc % 2
                act(S, tA[p][:], psA[:, 0:TN], AF.Tanh, [b_psA], [b_tA[p]], scale=0.5)
                act(S, tB[p][:], psB[:, 0:TN], AF.Tanh, [b_psB], [b_tB[p]], scale=0.5)
                stt(S, mA[p][:], tA[p][:], 1.0, yc[:, dc, :], ALU.add, ALU.mult, [b_tA[p], byc], [b_mA[p]])
                stt(S, mB[p][:], tB[p][:], 1.0, psD[:, 0:TN], ALU.add, ALU.mult, [b_tB[p], b_psD], [b_mB[p]])
                tt(S, "gpsimd", mt[:, dc, :], mA[p][:], mB[p][:], ALU.add, [b_mA[p], b_mB[p]], [b_m])
            for dc in range(KC):
                po = psO[dc % 2]
                for c in range(KC):
                    mm(S, po[:, 0:TN], WO[:, c, dc * 128:(dc + 1) * 128], mt[:, c, :], c == 0, c == KC - 1,
                       [b_wo, b_m], [b_psO[dc % 2]])
                stt(S, h[:, dc, :], po[:, 0:TN], g.mods[:, l, 5 * 8 + dc, col:col + 1], h[:, dc, :], ALU.mult, ALU.add,
                    [b_psO[dc % 2], bh], [bh])
            S.dma("sync", g.H[b, :, :, ti * TN:(ti + 1) * TN], h[:], reads=[bh], sem_buf=bh)
        S.barrier()
        S.flush()
        S.release([b_wz, b_wg, b_dp, b_wo] + b_h + b_ot + b_yc)


def stage_final(g):
    nc, S = g.nc, g.S
    with ExitStack() as st:
        sb = lambda n, s, d=F32: st.enter_context(_sbt(nc, "z_" + n, list(s), d))
        hb = [sb("h%d" % i, [128, KC, TN]) for i in range(2)]
        b_h = [Buf("h0"), Buf("h1")]
        ob = [sb("o%d" % i, [128, KC, TN]) for i in range(2)]
        b_o = [Buf("o0"), Buf("o1")]
        r = alloc_norm(g, st, "z_")
        tiles = tile_list(g, False)

        def load(i):
            b, ti = tiles[i]
            S.dma("sync", hb[i % 2][:], g.H[b, :, :, ti * TN:(ti + 1) * TN], writes=[b_h[i % 2]])

        load(0)
        for i, (b, ti) in enumerate(tiles):
            h, bh = hb[i % 2], b_h[i % 2]
            if i + 1 < len(tiles):
                load(i + 1)
            tt(S, "gpsimd", r.sq[:], h[:], h[:], ALU.mult, [bh], [r.b_sq])
            for c in range(KC):
                mm(S, r.pss[:, 0:TN], g.ones_bf[:], r.sq[:, c, :], c == 0, c == KC - 1, [r.b_sq, g.b_const], [r.b_pss])
            act(S, r.lr[:], r.pss[:, 0:TN], AF.Ln, [r.b_pss, g.b_const], [r.b_lr], scale=1.0 / D, bias=g.cvals[:, 0:1])
            act(S, r.rstd[:], r.lr[:], AF.Exp, [r.b_lr], [r.b_rstd], scale=-0.5)
            tt(S, "vector", r.u[:], h[:], r.rstd[:].unsqueeze(1).broadcast_to([128, KC, TN]), ALU.mult,
               [bh, r.b_rstd], [r.b_u])
            o = ob[i % 2]
            for c in range(KC):
                act(S, o[:, c, :], r.u[:, c, :], AF.Identity, [r.b_u, g.b_const], [b_o[i % 2]], scale=g.fin[:, c:c + 1])
            S.dma("sync", g.outT[b, :, :, (ti - 1) * TN:ti * TN], o[:], reads=[b_o[i % 2]], sem_buf=b_o[i % 2])
        S.barrier()
        S.flush()
        S.release(b_h + b_o)


def fm(v):
    v = np.asarray(v, np.float32)
    return np.ascontiguousarray(v.reshape(-1, 128).T)


def wl(w):
    w = np.asarray(w, np.float32)
    K, N = w.shape
    return np.ascontiguousarray(w.reshape(K // 128, 128, N).transpose(1, 0, 2))


def make_consts():
    i = np.arange(64)
    cm = np.zeros((64, 10, 64), np.float32)
    jj, ii = i[:, None], i[None, :]
    cm[:, 6, :] = (jj // 8 == ii // 8) & (jj != ii)
    for lv, sz in enumerate((8, 16, 32)):
        cm[:, 7 + lv, :] = (jj // (2 * sz) == ii // (2 * sz)) & (jj // sz != ii // sz)
    cm[:, 0, :] = (i[:, None] <= i[None, :])
    cm[:, 1, :] = (i[:, None] >= i[None, :])
    cm[:, 2, :] = np.where(i[None, :] >= i[:, None], 0.0, -1e30)
    cm[:, 3, :] = np.where(i[None, :] <= i[:, None], 0.0, -1e30)
    cm[:, 4, :] = (i[None, :] > i[:, None])
    cm[:, 5, :] = (i[None, :] < i[:, None])
    return np.eye(128, dtype=np.float32), cm


def prep_shared(inp, L):
    sh = {}
    sh["adaw"] = np.stack([wl(inp["ada_w"][l]) for l in range(L)])
    sh["adab"] = np.stack([fm(inp["ada_b"][l]) for l in range(L)])
    vecs = np.zeros((128, L, NV), np.float32)
    for l in range(L):
        vecs[:, l, V_N1:V_N1 + 8] = fm(inp["ffn1_norm"][l])
        vecs[:, l, V_NM:V_NM + 8] = fm(inp["mix_norm"][l])
        vecs[:, l, V_N2:V_N2 + 8] = fm(inp["ffn2_norm"][l])
        vecs[:, l, V_CB:V_CB + 8] = fm(inp["conv_dw_b"][l])
        vecs[:, l, V_LG:V_LG + 8] = fm(inp["conv_ln_g"][l])
        vecs[:, l, V_LB:V_LB + 8] = fm(inp["conv_ln_b"][l])
        vecs[:, l, V_ON] = np.asarray(inp["dn_onorm"][l], np.float32)
        dw = np.asarray(inp["conv_dw"][l], np.float32)
        vecs[:, l, V_DW:V_DW + 248] = dw.reshape(31, 8, 128).transpose(2, 1, 0).reshape(128, 248)
        shw = np.asarray(inp["dn_short"][l], np.float32)
        vecs[:, l, V_SH:V_SH + 72] = shw.reshape(3, 24, 128).transpose(2, 1, 0).reshape(128, 72)
    sh["vecs"] = vecs
    sh["fin"] = fm(inp["final_norm"])
    dnc = np.zeros((64, L, 2, 16), np.float32)
    for l in range(L):
        dnc[:, l, 0, :] = np.asarray(inp["dn_a_log"][l], np.float32).reshape(1, 16)
        dnc[:, l, 1, :] = np.asarray(inp["dn_dt_bias"][l], np.float32).reshape(1, 16)
    sh["dnc"] = dnc
    sh["w13a"] = np.stack([wl(inp["ffn1_w13"][l]) for l in range(L)])
    sh["w13b"] = np.stack([wl(inp["ffn2_w13"][l]) for l in range(L)])
    sh["w2a"] = np.stack([wl(inp["ffn1_w2"][l]) for l in range(L)])
    sh["w2b"] = np.stack([wl(inp["ffn2_w2"][l]) for l in range(L)])
    sh["win"] = np.stack([wl(inp["w_in"][l]) for l in range(L)])
    sh["cproj"] = np.stack([wl(inp["conv_proj"][l]) for l in range(L)])
    sh["dproj"] = np.stack([wl(inp["dn_proj"][l]) for l in range(L)])
    sh["wout"] = np.stack([wl(inp["w_out"][l]) for l in range(L)])
    sh["ident"], sh["cmask"] = make_consts()
    return sh


def prep_core(inp, b0, NBL):
    x = np.asarray(inp["x"][b0:b0 + NBL], np.float32)
    cx = np.asarray(inp["ctx"][b0:b0 + NBL], np.float32)
    hcat = np.concatenate([cx, x], axis=1)
    hT0 = np.ascontiguousarray(hcat.reshape(NBL, T, KC, 128).transpose(0, 3, 2, 1))
    cT = np.zeros((128, KC, 5), np.float32)
    cc = np.asarray(inp["c"][b0:b0 + NBL], np.float32)
    cT[:, :, 0:NBL] = cc.reshape(NBL, KC, 128).transpose(2, 1, 0)
    cT[:, :, 4] = np.asarray(inp["c_ctx"], np.float32).reshape(KC, 128).T
    return {"hT0": hT0, "cT": cT}


_PROG = {}
NB_LAUNCH = 4


def kernel(**inputs):
    L, NBL = DEPTH, NB_LAUNCH
    per_core = BATCH // NCORES
    if "nc" not in _PROG:
        _PROG["nc"] = build_program(L, NBL)
    nc = _PROG["nc"]
    sh = prep_shared(inputs, L)
    out = np.zeros((BATCH, SEQ, D), np.float32)
    for r in range(per_core // NBL):
        in_maps = []
        for c in range(NCORES):
            m = dict(sh)
            m.update(prep_core(inputs, c * per_core + r * NBL, NBL))
            in_maps.append(m)
        res = run_bass_kernel_spmd(nc, in_maps, core_ids=list(range(NCORES)))
        for c in range(NCORES):
            oT = np.asarray(res.results[c]["outT"])
            b0 = c * per_core + r * NBL
            out[b0:b0 + NBL] = oT.transpose(0, 3, 2, 1).reshape(NBL, SEQ, D)
    return out
```

```python
from contextlib import ExitStack
import numpy as np
import concourse.bass as bass
import concourse.mybir as mybir
from concourse.bass_utils import run_bass_kernel_spmd

F32 = mybir.dt.float32
BF16 = mybir.dt.bfloat16
ALU = mybir.AluOpType
AF = mybir.ActivationFunctionType
AX = mybir.AxisListType

COMPUTE = ("tensor", "vector", "scalar", "gpsimd")


class Buf:
    __slots__ = ("name", "w", "r", "dkey")

    def __init__(self, name):
        self.name = name
        self.w = {}
        self.r = {}
        self.dkey = None


class Op:
    __slots__ = ("eng", "fn", "deps", "key", "val", "needed", "is_dma")


class Sched:
    def __init__(self, nc, stack):
        self.nc = nc
        self.ops = []
        self.nd = 0
        self.flushed = 0
        self.sems = {}
        self.cnt = {}
        self.seen = {e: {} for e in COMPUTE + ("sync",)}
        self.lastop = {}
        self.bar = 0
        self._stack = stack
        self.free_dkeys = []

    def _sem(self, key):
        if key not in self.sems:
            self.sems[key] = self._stack.enter_context(self.nc.semaphore("s_" + str(key)))
            self.cnt[key] = 0
        return self.sems[key]

    def _add(self, eng, fn, reads, writes, parts, key, is_dma):
        op = Op()
        op.eng = eng
        op.fn = fn
        op.key = key
        op.is_dma = is_dma
        op.needed = is_dma
        op.val = None
        oid = len(self.ops)
        deps = {}
        own = eng if eng in COMPUTE else None
        bar = self.bar
        for b in reads:
            for k, o in b.w.items():
                if k == own and eng == "tensor":
                    continue
                if o >= bar and deps.get(k, -1) < o:
                    deps[k] = o
        for b in writes:
            for k, o in b.w.items():
                if k == own or k == key:
                    continue
                if o >= bar and deps.get(k, -1) < o:
                    deps[k] = o
            for k, o in b.r.items():
                if k == own:
                    continue
                if o >= bar and deps.get(k, -1) < o:
                    deps[k] = o
        for b in parts:
            for k, o in b.r.items():
                if k == own:
                    continue
                if o >= bar and deps.get(k, -1) < o:
                    deps[k] = o
        op.deps = deps
        for o in deps.values():
            self.ops[o].needed = True
        self.ops.append(op)
        for b in reads:
            b.r[key] = oid
        for b in writes:
            b.w[key] = oid
        for b in parts:
            b.w[key] = oid
        self.lastop[key] = oid
        return oid

    def op(self, eng, fn, reads=(), writes=(), parts=()):
        self._sem(eng)
        return self._add(eng, fn, reads, writes, parts, eng, False)

    def dma(self, eng, out_ap, in_ap, reads=(), writes=(), parts=(), sem_buf=None):
        if sem_buf is None:
            sem_buf = (list(writes) + list(parts))[0]
        if sem_buf.dkey is None:
            if self.free_dkeys:
                sem_buf.dkey = self.free_dkeys.pop()
            else:
                sem_buf.dkey = "d%d" % self.nd
                self.nd += 1
        self._sem(sem_buf.dkey)

        def fn(e, o=out_ap, i=in_ap):
            return e.dma_start(out=o, in_=i)
        return self._add(eng, fn, reads, writes, parts, sem_buf.dkey, True)

    def release(self, bufs):
        for b in bufs:
            if b.dkey is not None:
                self.free_dkeys.append(b.dkey)
                b.dkey = None

    def barrier(self):
        allk = dict(self.lastop)
        for eng in COMPUTE + ("sync",):
            op = Op()
            op.eng = eng
            op.fn = None
            op.key = None
            op.is_dma = False
            op.needed = False
            op.val = None
            op.deps = {k: o for k, o in allk.items() if k != eng and o >= self.bar}
            for o in op.deps.values():
                self.ops[o].needed = True
            self.ops.append(op)
        self.bar = len(self.ops)

    def flush(self):
        ops = self.ops[self.flushed:]
        self.flushed = len(self.ops)
        for op in ops:
            if op.fn is None:
                continue
            if op.is_dma:
                self.cnt[op.key] += 16
                op.val = self.cnt[op.key]
            elif op.needed:
                self.cnt[op.key] += 1
                op.val = self.cnt[op.key]
        streams = {e: [] for e in COMPUTE + ("sync",)}
        for op in ops:
            streams[op.eng].append(op)
        allops = self.ops
        sems = self.sems
        seen = self.seen

        def run(eng_name, e):
            sn = seen[eng_name]
            for op in streams[eng_name]:
                for k, o in op.deps.items():
                    v = allops[o].val
                    assert v is not None, (eng_name, k, o)
                    if sn.get(k, 0) < v:
                        e.wait_ge(sems[k], v)
                        sn[k] = v
                if op.fn is None:
                    continue
                ins = op.fn(e)
                if op.val is not None:
                    ins.then_inc(sems[op.key], 16 if op.is_dma else 1)

        with self.nc.Block() as block:
            if streams["sync"]:
                @block.sync
                def _(e):
                    run("sync", e)
            if streams["tensor"]:
                @block.tensor
                def _(e):
                    run("tensor", e)
            if streams["vector"]:
                @block.vector
                def _(e):
                    run("vector", e)
            if streams["scalar"]:
                @block.scalar
                def _(e):
                    run("scalar", e)
            if streams["gpsimd"]:
                @block.gpsimd
                def _(e):
                    run("gpsimd", e)
        for op in ops:
            op.fn = None if op.fn is None else True


D = 1024
KC = 8
SEQ = 2048
CTXL = 256
T = SEQ + CTXL
NCH = T // 64
DEPTH = 4
NCORES = 8
BATCH = 32
DFF = 2816
NJ = DFF // 128
NIN = 8224
O_CONV, O_QKV, O_Z, O_AB, O_GATE = 0, 2048, 5120, 6144, 6176
TN = 256
NT = T // TN
EPS = 1e-6
TP = T + 3

V_N1, V_NM, V_N2, V_CB, V_LG, V_LB, V_ON, V_DW, V_SH = 0, 8, 16, 24, 32, 40, 48, 49, 49 + 248
NV = 49 + 248 + 72


def ptok(c):
    return 1 + 64 * c if c < 4 else 2 + 64 * c


class Ctx:
    pass


_UID = [0]


def _sbt(nc, name, shape, dt):
    _UID[0] += 1
    return nc.sbuf_tensor("%s_%d" % (name, _UID[0]), shape, dt)


def _pst(nc, name, shape, dt):
    _UID[0] += 1
    return nc.psum_tensor("%s_%d" % (name, _UID[0]), shape, dt)


def build_program(L=DEPTH, NBL=4, stop_after=None, debug=False, only=None, dn_phase=None):
    nc = bass.Bass("TRN2", target_bir_lowering=False)
    dk = "ExternalOutput" if debug else "Internal"

    def din(name, shape, dt=F32):
        if only == "dn" and name not in ("vecs", "fin", "dnc", "ident", "cmask"):
            return None
        return nc.dram_tensor(name, list(shape), dt, kind="ExternalInput").ap()

    g = Ctx()
    g.nc = nc
    g.L, g.NBL = L, NBL
    g.hT0 = din("hT0", [NBL, 128, KC, T])
    g.cT = din("cT", [128, KC, 5])
    g.adaw = din("adaw", [L, 128, KC, 9 * D])
    g.adab = din("adab", [L, 128, 72])
    g.vecs_d = din("vecs", [128, L, NV])
    g.fin_d = din("fin", [128, KC])
    g.dnc_d = din("dnc", [64, L, 2, 16])
    g.w13 = [din("w13a", [L, 128, KC, 2 * DFF]), din("w13b", [L, 128, KC, 2 * DFF])]
    g.w2 = [din("w2a", [L, 128, NJ, D]), din("w2b", [L, 128, NJ, D])]
    g.win = din("win", [L, 128, KC, NIN])
    g.cproj = din("cproj", [L, 128, KC, D])
    g.dproj = din("dproj", [L, 128, KC, D])
    g.wout = din("wout", [L, 128, KC, D])
    g.ident_d = din("ident", [128, 128])
    g.cm_d = din("cmask", [64, 10, 64])
    g.outT = nc.dram_tensor("outT", [NBL, 128, KC, SEQ], F32, kind="ExternalOutput").ap()
    g.H = nc.dram_tensor("Hs", [NBL, 128, KC, T], F32, kind=dk).ap()
    g.PQ = nc.dram_tensor("PQs", [NBL, 24, 128, T], F32, kind="ExternalInput" if only == "dn" else dk).ap()
    g.AB = nc.dram_tensor("ABs", [NBL, T, 32], F32, kind="ExternalInput" if only == "dn" else dk).ap()
    g.dn_phase = dn_phase
    g.OT = nc.dram_tensor("OTs", [NBL, 128, KC, T], BF16, kind=dk).ap()
    g.MODS = nc.dram_tensor("MODs", [128, L, 72, 5], F32, kind=dk).ap()
    g.YC = nc.dram_tensor("YCs", [NBL, 128, KC, T], BF16, kind=dk).ap()

    with ExitStack() as stack:
        S = Sched(nc, stack)
        g.S = S
        g.stack = stack
        sb = lambda n, s, d=F32: stack.enter_context(_sbt(nc, n, list(s), d))
        g.mods = sb("mods", [128, L, 72, 5])
        g.vecs = sb("vecs_sb", [128, L, NV])
        g.fin = sb("fin_sb", [128, KC])
        g.dnc = sb("dnc_sb", [64, L, 2, 16])
        g.ones_bf = sb("ones_bf", [128, 128], BF16)
        g.ones_f = sb("ones_f", [64, 128])
        g.ident = sb("ident_bf", [128, 128], BF16)
        g.cm = sb("cm_sb", [64, 10, 64])
        g.cvals = sb("cvals", [128, 4])
        g.b_const = Buf("const")
        cb = g.b_const
        S.dma("sync", g.vecs[:], g.vecs_d, parts=[cb])
        S.dma("sync", g.fin[:], g.fin_d, parts=[cb])
        S.dma("sync", g.dnc[:], g.dnc_d, parts=[cb])
        S.dma("sync", g.cm[:], g.cm_d, parts=[cb])
        S.dma("gpsimd", g.ident[:], g.ident_d, parts=[cb])
        S.op("gpsimd", lambda e: e.memset(g.ones_bf[:], 1.0), parts=[cb])
        S.op("gpsimd", lambda e: e.memset(g.ones_f[:], 1.0), parts=[cb])
        S.op("gpsimd", lambda e: e.memset(g.cvals[:, 0:1], EPS), parts=[cb])
        S.op("gpsimd", lambda e: e.memset(g.cvals[:, 1:2], 4.0 * EPS), parts=[cb])
        S.op("gpsimd", lambda e: e.memset(g.cvals[:, 2:3], 1.0), parts=[cb])
        S.op("gpsimd", lambda e: e.memset(g.cvals[:, 3:4], 0.0), parts=[cb])
        S.barrier()
        S.flush()

        stages = []
        if only == "dn":
            stage_dn(g, 0)
            S.barrier()
            S.flush()
            return nc
        stages.append(("mods", lambda: stage_mods(g)))
        for l in range(L):
            last = l == L - 1
            stages.append(("ffn1_%d" % l, lambda l=l: stage_ffn(g, l, 0, True)))
            stages.append(("m1_%d" % l, lambda l=l: stage_m1(g, l)))
            stages.append(("conv_%d" % l, lambda l=l, last=last: stage_conv(g, l, not last)))
            stages.append(("dn_%d" % l, lambda l=l: stage_dn(g, l)))
            stages.append(("m3_%d" % l, lambda l=l, last=last: stage_m3(g, l, not last)))
            stages.append(("ffn2_%d" % l, lambda l=l, last=last: stage_ffn(g, l, 1, not last)))
        stages.append(("final", lambda: stage_final(g)))
        for name, fn in stages:
            fn()
            if stop_after == name:
                break
        S.barrier()
        S.flush()
    return nc


def mm(S, out, lhsT, rhs, start, stop, R, W):
    S.op("tensor", lambda e: e.matmul(out, lhsT=lhsT, rhs=rhs, start=start, stop=stop), R, W)


def tr(S, out, in_, ident, R, W):
    S.op("tensor", lambda e: e.matmul(out, lhsT=in_, rhs=ident, start=True, stop=True), R, W)


def act(S, out, in_, func, R, W, scale=None, bias=None):
    kw = {}
    if scale is not None:
        kw["scale"] = scale
    if bias is not None:
        kw["bias"] = bias
    S.op("scalar", lambda e: e.activation(out=out, in_=in_, func=func, **kw), R, W)


def tt(S, eng, out, in0, in1, op, R, W):
    S.op(eng, lambda e: e.tensor_tensor(out=out, in0=in0, in1=in1, op=op), R, W)


def ts(S, eng, out, in0, s1, s2, op0, op1, R, W):
    if s2 is None:
        S.op(eng, lambda e: e.tensor_scalar(out=out, in0=in0, scalar1=s1, scalar2=None, op0=op0), R, W)
    else:
        S.op(eng, lambda e: e.tensor_scalar(out=out, in0=in0, scalar1=s1, scalar2=s2, op0=op0, op1=op1), R, W)


def stt(S, out, in0, scalar, in1, op0, op1, R, W):
    S.op("vector", lambda e: e.scalar_tensor_tensor(out=out, in0=in0, scalar=scalar, in1=in1, op0=op0, op1=op1), R, W)


def cp(S, eng, out, in_, R, W):
    if eng == "scalar":
        S.op("scalar", lambda e: e.activation(out=out, in_=in_, func=AF.Copy), R, W)
    else:
        S.op(eng, lambda e: e.tensor_copy(out=out, in_=in_), R, W)


def load_w(S, dst, src, buf, nsplit, axis=1):
    n = dst.shape[axis]
    step = (n + nsplit - 1) // nsplit
    for i in range(0, n, step):
        j = min(n, i + step)
        if axis == 1:
            S.dma("gpsimd", dst[:, i:j], src[:, i:j], parts=[buf])
        else:
            S.dma("gpsimd", dst[:, :, i:j], src[:, :, i:j], parts=[buf])


def stage_mods(g):
    nc, S, L = g.nc, g.S, g.L
    with ExitStack() as st:
        sb = lambda n, s, d=F32: st.enter_context(_sbt(nc, n, list(s), d))
        ct = sb("m_ct", [128, KC, 5])
        cs = sb("m_cs", [128, KC, 5], BF16)
        adb = sb("m_adb", [128, L, 72])
        wbuf = [sb("m_w%d" % i, [128, KC, 1152], BF16) for i in range(2)]
        ps = [st.enter_context(_pst(nc, "m_ps%d" % i, [128, 512], F32)) for i in range(2)]
        b_ct, b_cs, b_adb = Buf("ct"), Buf("cs"), Buf("adb")
        b_w = [Buf("mw0"), Buf("mw1")]
        b_ps = [Buf("mps0"), Buf("mps1")]
        b_mods = Buf("mods")
        S.dma("sync", ct[:], g.cT, writes=[b_ct])
        S.dma("sync", adb[:], g.adab.rearrange("l p j -> p l j"), writes=[b_adb])
        th = sb("m_th", [128, KC, 5])
        act(S, th[:], ct[:], AF.Tanh, [b_ct], [b_cs], scale=0.5)
        stt(S, th[:], th[:], 1.0, ct[:], ALU.add, ALU.mult, [b_ct, b_cs], [b_cs])
        ts(S, "vector", cs[:], th[:], 0.5, None, ALU.mult, None, [b_cs], [b_cs])
        blk = 0
        for l in range(L):
            p = ps[l % 2]
            bp = b_ps[l % 2]
            for nb in range(8):
                w = wbuf[blk % 2]
                bw = b_w[blk % 2]
                for half in range(2):
                    S.dma("gpsimd", w[:, half * 4:(half + 1) * 4, :],
                          g.adaw[l, :, half * 4:(half + 1) * 4, nb * 1152:(nb + 1) * 1152], writes=[bw])
                for jl in range(9):
                    j = nb * 9 + jl
                    for kc in range(KC):
                        mm(S, p[:, j * 5:(j + 1) * 5], w[:, kc, jl * 128:(jl + 1) * 128], cs[:, kc, :],
                           kc == 0, kc == KC - 1, [bw, b_cs], [bp])
                blk += 1
            tt(S, "vector", g.mods[:, l, :, :], p[:, 0:360].rearrange("p (j b) -> p j b", b=5),
               adb[:, l, :].unsqueeze(2).broadcast_to([128, 72, 5]), ALU.add, [bp, b_adb], [b_mods])
            for (ms, vo) in ((1, V_N1), (4, V_NM), (7, V_N2)):
                stt(S, g.mods[:, l, ms * 8:(ms + 1) * 8, :], g.mods[:, l, ms * 8:(ms + 1) * 8, :], 1.0,
                    g.vecs[:, l, vo:vo + 8].unsqueeze(2).broadcast_to([128, 8, 5]), ALU.add, ALU.mult,
                    [b_mods, g.b_const], [b_mods])
            for mg in (2, 5, 8):
                ts(S, "vector", g.mods[:, l, mg * 8:(mg + 1) * 8, :], g.mods[:, l, mg * 8:(mg + 1) * 8, :],
                   0.5, None, ALU.mult, None, [b_mods], [b_mods])
            ts(S, "vector", g.vecs[:, l, V_DW:V_DW + 248], g.vecs[:, l, V_DW:V_DW + 248], 0.5, None,
               ALU.mult, None, [g.b_const, b_mods], [g.b_const])
        S.dma("sync", g.MODS, g.mods[:], reads=[b_mods], sem_buf=b_mods)
        S.barrier()
        S.flush()
        S.release([b_ct, b_cs, b_adb, b_mods] + b_w)


class NormRes:
    pass


def alloc_norm(g, st, pfx):
    nc = g.nc
    sb = lambda n, s, d=F32: st.enter_context(_sbt(nc, pfx + n, list(s), d))
    r = NormRes()
    r.sq = sb("sq", [128, KC, TN], BF16)
    r.u = sb("u", [128, KC, TN])
    r.lr = sb("lr", [128, TN])
    r.rstd = sb("rstd", [128, TN])
    r.n = sb("n", [128, KC, TN], BF16)
    r.pss = st.enter_context(_pst(nc, pfx + "pss", [128, 512], F32))
    r.b_sq, r.b_u, r.b_lr, r.b_rstd, r.b_n, r.b_pss = [Buf(pfx + x) for x in ("sq", "u", "lr", "rstd", "n", "pss")]
    return r


def norm_mod(g, r, h, b_h, l, m_shift, m_A, col):
    S = g.S
    tt(S, "gpsimd", r.sq[:], h[:], h[:], ALU.mult, [b_h], [r.b_sq])
    for c in range(KC):
        mm(S, r.pss[:, 0:TN], g.ones_bf[:], r.sq[:, c, :], c == 0, c == KC - 1, [r.b_sq, g.b_const], [r.b_pss])
    act(S, r.lr[:], r.pss[:, 0:TN], AF.Ln, [r.b_pss, g.b_const], [r.b_lr], scale=1.0 / D, bias=g.cvals[:, 0:1])
    act(S, r.rstd[:], r.lr[:], AF.Exp, [r.b_lr], [r.b_rstd], scale=-0.5)
    tt(S, "vector", r.u[:], h[:], r.rstd[:].unsqueeze(1).broadcast_to([128, KC, TN]), ALU.mult,
       [b_h, r.b_rstd], [r.b_u])
    for c in range(KC):
        act(S, r.n[:, c, :], r.u[:, c, :], AF.Identity, [r.b_u], [r.b_n],
            scale=g.mods[:, l, m_A * 8 + c, col:col + 1], bias=g.mods[:, l, m_shift * 8 + c, col:col + 1])


def tile_list(g, with_ctx=True):
    return [(b, ti) for b in range(g.NBL) for ti in range(NT) if (with_ctx or ti > 0)]


def stage_ffn(g, l, which, with_ctx):
    nc, S = g.nc, g.S
    m_shift, m_A, m_gate = (0, 1, 2) if which == 0 else (6, 7, 8)
    src = g.hT0 if (l == 0 and which == 0) else g.H
    with ExitStack() as st:
        sb = lambda n, s, d=F32: st.enter_context(_sbt(nc, "f_" + n, list(s), d))
        W13 = sb("w13", [128, KC, 2 * DFF], BF16)
        W2 = sb("w2", [128, NJ, D], BF16)
        b_w13, b_w2 = Buf("w13"), Buf("w2")
        load_w(S, W13[:], g.w13[which][l], b_w13, 8, axis=1)
        load_w(S, W2[:], g.w2[which][l], b_w2, 4, axis=1)
        hb = [sb("h%d" % i, [128, KC, TN]) for i in range(2)]
        b_h = [Buf("h0"), Buf("h1")]
        r = alloc_norm(g, st, "f_")
        actb = sb("act", [128, NJ, TN], BF16)
        b_act = Buf("act")
        sa = [sb("sa%d" % i, [128, TN]) for i in range(2)]
        b_sa = [Buf("sa0"), Buf("sa1")]
        psA = [st.enter_context(_pst(nc, "f_psA%d" % i, [128, 512], F32)) for i in range(2)]
        psB = [st.enter_context(_pst(nc, "f_psB%d" % i, [128, 512], F32)) for i in range(2)]
        psO = [st.enter_context(_pst(nc, "f_psO%d" % i, [128, 512], F32)) for i in range(2)]
        b_psA = [Buf("psA0"), Buf("psA1")]
        b_psB = [Buf("psB0"), Buf("psB1")]
        b_psO = [Buf("psO0"), Buf("psO1")]
        tiles = tile_list(g, with_ctx)

        def load(i):
            b, ti = tiles[i]
            S.dma("sync", hb[i % 2][:], src[b, :, :, ti * TN:(ti + 1) * TN], writes=[b_h[i % 2]])

        def norm(i):
            b, ti = tiles[i]
            col = 4 if ti == 0 else b
            norm_mod(g, r, hb[i % 2], b_h[i % 2], l, m_shift, m_A, col)

        load(0)
        norm(0)
        for i, (b, ti) in enumerate(tiles):
            h = hb[i % 2]
            bh = b_h[i % 2]
            col = 4 if ti == 0 else b
            if i + 1 < len(tiles):
                load(i + 1)
            if True:
                for j in range(NJ):
                    pa, pb = psA[j % 2], psB[j % 2]
                    for kc in range(KC):
                        mm(S, pa[:, 0:TN], W13[:, kc, j * 128:(j + 1) * 128], r.n[:, kc, :], kc == 0, kc == KC - 1,
                           [b_w13, r.b_n], [b_psA[j % 2]])
                    for kc in range(KC):
                        mm(S, pb[:, 0:TN], W13[:, kc, DFF + j * 128:DFF + (j + 1) * 128], r.n[:, kc, :], kc == 0,
                           kc == KC - 1, [b_w13, r.b_n], [b_psB[j % 2]])
                    act(S, sa[j % 2][:], pa[:, 0:TN], AF.Silu, [b_psA[j % 2]], [b_sa[j % 2]])
                    tt(S, "vector", actb[:, j, :], sa[j % 2][:], pb[:, 0:TN], ALU.mult, [b_sa[j % 2], b_psB[j % 2]],
                       [b_act])
            if i + 1 < len(tiles):
                norm(i + 1)
            if True:
                for dc in range(KC):
                    po = psO[dc % 2]
                    for j in range(NJ):
                        mm(S, po[:, 0:TN], W2[:, j, dc * 128:(dc + 1) * 128], actb[:, j, :], j == 0, j == NJ - 1,
                           [b_w2, b_act], [b_psO[dc % 2]])
                    stt(S, h[:, dc, :], po[:, 0:TN], g.mods[:, l, m_gate * 8 + dc, col:col + 1], h[:, dc, :],
                        ALU.mult, ALU.add, [b_psO[dc % 2], bh], [bh])
            S.dma("sync", g.H[b, :, :, ti * TN:(ti + 1) * TN], h[:], reads=[bh], sem_buf=bh)
        S.barrier()
        S.flush()
        S.release([b_w13, b_w2] + b_h)


def stage_m1(g, l):
    nc, S = g.nc, g.S
    with ExitStack() as st:
        sb = lambda n, s, d=F32: st.enter_context(_sbt(nc, "a_" + n, list(s), d))
        WQ = sb("wq", [128, KC, 3072], BF16)
        WAB = sb("wab", [128, KC, 32], BF16)
        b_wq, b_wab = Buf("wq"), Buf("wab")
        load_w(S, WQ[:], g.win[l][:, :, O_QKV:O_Z], b_wq, 8, axis=1)
        S.dma("gpsimd", WAB[:], g.win[l][:, :, O_AB:O_GATE], writes=[b_wab])
        hb = [sb("h%d" % i, [128, KC, TN]) for i in range(2)]
        b_h = [Buf("h0"), Buf("h1")]
        r = alloc_norm(g, st, "a_")
        stg = [sb("stg%d" % i, [128, 4, TN]) for i in range(2)]
        b_stg = [Buf("stg0"), Buf("stg1")]
        abst = [sb("abst%d" % i, [128, TN // 128, 32]) for i in range(2)]
        b_abst = [Buf("abst0"), Buf("abst1")]
        ps = [st.enter_context(_pst(nc, "a_ps%d" % i, [128, 512], F32)) for i in range(4)]
        b_ps = [Buf("aps%d" % i) for i in range(4)]
        psab = st.enter_context(_pst(nc, "a_psab", [128, 512], F32))
        b_psab = Buf("psab")
        tiles = tile_list(g)

        def load(i):
            b, ti = tiles[i]
            S.dma("sync", hb[i % 2][:], g.H[b, :, :, ti * TN:(ti + 1) * TN], writes=[b_h[i % 2]])

        load(0)
        for i, (b, ti) in enumerate(tiles):
            h = hb[i % 2]
            col = 4 if ti == 0 else b
            if i + 1 < len(tiles):
                load(i + 1)
            norm_mod(g, r, h, b_h[i % 2], l, 3, 4, col)
            for j in range(24):
                p = ps[j % 4]
                for kc in range(KC):
                    mm(S, p[:, 0:TN], WQ[:, kc, j * 128:(j + 1) * 128], r.n[:, kc, :], kc == 0, kc == KC - 1,
                       [b_wq, r.b_n], [b_ps[j % 4]])
                sg = stg[(j // 4) % 2]
                bsg = b_stg[(j // 4) % 2]
                cp(S, "scalar" if j % 2 == 0 else "vector", sg[:, j % 4, :], p[:, 0:TN], [b_ps[j % 4]], [bsg])
                if j % 4 == 3:
                    S.dma("sync", g.PQ[b, j - 3:j + 1, :, ti * TN:(ti + 1) * TN].rearrange("c p t -> p c t"),
                          sg[:], reads=[bsg], sem_buf=bsg)
            ab = abst[i % 2]
            for tb in range(TN // 128):
                for kc in range(KC):
                    mm(S, psab[:, tb * 32:(tb + 1) * 32], r.n[:, kc, tb * 128:(tb + 1) * 128], WAB[:, kc, :],
                       kc == 0, kc == KC - 1, [b_wab, r.b_n], [b_psab])
            cp(S, "vector", ab[:], psab[:, 0:(TN // 128) * 32].rearrange("p (a f) -> p a f", f=32), [b_psab],
               [b_abst[i % 2]])
            S.dma("sync", g.AB[b, ti * TN:(ti + 1) * TN, :].rearrange("(a p) f -> p a f", p=128), ab[:],
                  reads=[b_abst[i % 2]], sem_buf=b_abst[i % 2])
        S.barrier()
        S.flush()
        S.release([b_wq, b_wab] + b_h + b_stg + b_abst)


def stage_conv(g, l, with_ctx):
    nc, S = g.nc, g.S
    with ExitStack() as st:
        sb = lambda n, s, d=F32: st.enter_context(_sbt(nc, "c_" + n, list(s), d))
        WC = sb("wc", [128, KC, 2048], BF16)
        CP = sb("cp", [128, KC, D], BF16)
        b_wc, b_cp = Buf("wc"), Buf("cpw")
        load_w(S, WC[:], g.win[l][:, :, O_CONV:O_QKV], b_wc, 8, axis=1)
        load_w(S, CP[:], g.cproj[l], b_cp, 4, axis=1)
        hb = [sb("h%d" % i, [128, KC, TN]) for i in range(2)]
        b_h = [Buf("h0"), Buf("h1")]
        r = alloc_norm(g, st, "c_")
        ypL = sb("ypL", [128, KC, 4, 94], BF16)
        ypC = sb("ypC", [128, KC, 286], BF16)
        dg = sb("dg", [128, KC, 31, 128], BF16)
        b_dg = Buf("dg")
        b_yp = Buf("yp")
        tg = [sb("tg%d" % i, [128, TN]) for i in range(2)]
        b_tg = [Buf("tg0"), Buf("tg1")]
        cv = sb("cv", [128, KC, TN])
        b_cv = Buf("cv")
        xb = sb("xb", [128, KC, TN], BF16)
        sqb = sb("sqb", [128, KC, TN], BF16)
        b_xb, b_sqb = Buf("xb"), Buf("sqb")
        mean = sb("mean", [128, TN])
        msq = sb("msq", [128, TN])
        var = sb("var", [128, TN])
        rs2 = sb("rs2", [128, TN])
        b_st = Buf("stats")
        sact = sb("sact", [128, KC, TN], BF16)
        b_sact = Buf("sact")
        yst = [sb("yst%d" % i, [128, KC, TN], BF16) for i in range(2)]
        b_yst = [Buf("yst0"), Buf("yst1")]
        psA = [st.enter_context(_pst(nc, "c_psA%d" % i, [128, 512], F32)) for i in range(2)]
        psB = [st.enter_context(_pst(nc, "c_psB%d" % i, [128, 512], F32)) for i in range(2)]
        pst = st.enter_context(_pst(nc, "c_pst", [128, 512], F32))
        psC = [st.enter_context(_pst(nc, "c_psC%d" % i, [128, 512], F32)) for i in range(2)]
        b_psA = [Buf("cpsA0"), Buf("cpsA1")]
        b_psB = [Buf("cpsB0"), Buf("cpsB1")]
        b_pst = Buf("cpst")
        b_psC = [Buf("cpsC0"), Buf("cpsC1")]
        S.op("gpsimd", lambda e: e.memset(ypL[:], 0.0), writes=[b_yp])
        S.op("gpsimd", lambda e: e.memset(ypC[:], 0.0), writes=[b_yp])
        for c in range(KC):
            for k in range(31):
                ts(S, "vector" if (c * 31 + k) % 2 == 0 else "gpsimd", dg[:, c, k, :], g.ident[:],
                   g.vecs[:, l, V_DW + c * 31 + k:V_DW + c * 31 + k + 1], None, ALU.mult, None, [g.b_const], [b_dg])
        tiles = tile_list(g, with_ctx)

        def load(i):
            b, ti = tiles[i]
            S.dma("sync", hb[i % 2][:], g.H[b, :, :, ti * TN:(ti + 1) * TN], writes=[b_h[i % 2]])

        load(0)
        for i, (b, ti) in enumerate(tiles):
            h = hb[i % 2]
            col = 4 if ti == 0 else b
            ctx = ti == 0
            if i + 1 < len(tiles):
                load(i + 1)
            norm_mod(g, r, h, b_h[i % 2], l, 3, 4, col)
            for c in range(KC):
                pa, pb = psA[c % 2], psB[c % 2]
                for kc in range(KC):
                    mm(S, pa[:, 0:TN], WC[:, kc, c * 128:(c + 1) * 128], r.n[:, kc, :], kc == 0, kc == KC - 1,
                       [b_wc, r.b_n], [b_psA[c % 2]])
                for kc in range(KC):
                    mm(S, pb[:, 0:TN], WC[:, kc, D + c * 128:D + (c + 1) * 128], r.n[:, kc, :], kc == 0, kc == KC - 1,
                       [b_wc, r.b_n], [b_psB[c % 2]])
                act(S, tg[c % 2][:], pb[:, 0:TN], AF.Tanh, [b_psB[c % 2]], [b_tg[c % 2]], scale=0.5)
                if ctx:
                    stt(S, ypC[:, c, 15:15 + TN], tg[c % 2][:], 1.0, pa[:, 0:TN], ALU.add, ALU.mult,
                        [b_tg[c % 2], b_psA[c % 2]], [b_yp])
                else:
                    stt(S, ypL[:, c, :, 15:79], tg[c % 2][:].rearrange("p (r w) -> p r w", w=64), 1.0,
                        pa[:, 0:TN].rearrange("p (r w) -> p r w", w=64), ALU.add, ALU.mult,
                        [b_tg[c % 2], b_psA[c % 2]], [b_yp])
            for c in range(KC):
                pv, bpv = psA[c % 2], b_psA[c % 2]
                if ctx:
                    o_ps = pv[:, 0:TN]
                    src = lambda k, c=c: ypC[:, c, k:k + TN]
                else:
                    o_ps = pv[:, 0:TN].rearrange("p (r w) -> p r w", w=64)
                    src = lambda k, c=c: ypL[:, c, :, k:k + 64]
                for k in range(31):
                    mm(S, o_ps, dg[:, c, k, :], src(k), k == 0, k == 30, [b_dg, b_yp], [bpv])
                act(S, cv[:, c, :], pv[:, 0:TN], AF.Identity, [bpv, g.b_const], [b_cv],
                    bias=g.vecs[:, l, V_CB + c:V_CB + c + 1])
            cp(S, "gpsimd", xb[:], cv[:], [b_cv], [b_xb])
            tt(S, "gpsimd", sqb[:], cv[:], cv[:], ALU.mult, [b_cv], [b_sqb])
            for c in range(KC):
                mm(S, pst[:, 0:TN], g.ones_bf[:], xb[:, c, :], c == 0, c == KC - 1, [b_xb, g.b_const], [b_pst])
            for c in range(KC):
                mm(S, pst[:, TN:2 * TN], g.ones_bf[:], sqb[:, c, :], c == 0, c == KC - 1, [b_sqb, g.b_const], [b_pst])
            act(S, mean[:], pst[:, 0:TN], AF.Identity, [b_pst], [b_st], scale=1.0 / D)
            tt(S, "gpsimd", msq[:], mean[:], mean[:], ALU.mult, [b_st], [b_st])
            stt(S, var[:], pst[:, TN:2 * TN], 1.0 / D, msq[:], ALU.mult, ALU.subtract, [b_pst, b_st], [b_st])
            act(S, var[:], var[:], AF.Ln, [b_st, g.b_const], [b_st], bias=g.cvals[:, 0:1])
            act(S, rs2[:], var[:], AF.Exp, [b_st], [b_st], scale=-0.5)
            tt(S, "vector", cv[:], cv[:], mean[:].unsqueeze(1).broadcast_to([128, KC, TN]), ALU.subtract,
               [b_cv, b_st], [b_cv])
            tt(S, "vector", cv[:], cv[:], rs2[:].unsqueeze(1).broadcast_to([128, KC, TN]), ALU.mult,
               [b_cv, b_st], [b_cv])
            for c in range(KC):
                act(S, sact[:, c, :], cv[:, c, :], AF.Silu, [b_cv, g.b_const], [b_sact],
                    scale=g.vecs[:, l, V_LG + c:V_LG + c + 1], bias=g.vecs[:, l, V_LB + c:V_LB + c + 1])
            ys = yst[i % 2]
            for dc in range(KC):
                pc = psC[dc % 2]
                for c in range(KC):
                    mm(S, pc[:, 0:TN], CP[:, c, dc * 128:(dc + 1) * 128], sact[:, c, :], c == 0, c == KC - 1,
                       [b_cp, b_sact], [b_psC[dc % 2]])
                cp(S, "scalar" if dc % 2 == 0 else "vector", ys[:, dc, :], pc[:, 0:TN], [b_psC[dc % 2]], [b_yst[i % 2]])
            S.dma("sync", g.YC[b, :, :, ti * TN:(ti + 1) * TN], ys[:], reads=[b_yst[i % 2]], sem_buf=b_yst[i % 2])
        S.barrier()
        S.flush()
        S.release([b_wc, b_cp] + b_h + b_yst)


HG = 2
NHD = 2 * HG


def stage_dn(g, l):
    nc, S = g.nc, g.S
    with ExitStack() as st:
        sb = lambda n, s, d=F32: st.enter_context(_sbt(nc, "d_" + n, list(s), d))
        B = lambda n: Buf("d_" + n)
        raw = sb("raw", [128, TP]); b_raw = B("raw")
        y = sb("y", [128, TP]); b_y = B("y")
        t = sb("t", [128, TP]); b_t = B("t")
        sqb = sb("sqb", [128, TP], BF16); b_sqb = B("sqb")
        qT = sb("qT", [128, HG, TP], BF16); b_qT = B("qT")
        kT = sb("kT", [128, HG, TP], BF16); b_kT = B("kT")
        vT = sb("vT", [128, HG, TP], BF16); b_vT = B("vT")
        ktok = sb("ktok", [64, NCH, HG, 128], BF16); b_ktok = B("ktok")
        vtok = sb("vtok", [64, NCH, HG, 128], BF16); b_vtok = B("vtok")
        oacc = sb("oacc", [64, NCH, HG, 128]); b_oacc = B("oacc")
        abraw = sb("abraw", [64, NCH, 32]); b_abraw = B("abraw")
        gg = sb("gg", [64, NCH, 16]); b_gg = B("gg")
        ghi = sb("ghi", [64, NCH, 16], BF16)
        glo = sb("glo", [64, NCH, 16], BF16)
        cmb = sb("cmb", [64, 10, 64], BF16)
        ones64 = g.ones_bf[0:64, :]
        beta = sb("beta", [64, NCH, 16]); b_beta = B("beta")
        tmpa = sb("tmpa", [64, NCH, 16]); b_tmpa = B("tmpa")
        nA = sb("nA", [64, 16]); b_nA = B("nA")
        Sst = sb("S", [128, NHD, 128]); b_S = [B("S%d" % i) for i in range(NHD)]
        Sbf = sb("Sbf", [128, NHD, 128], BF16); b_Sbf = [B("Sbf%d" % i) for i in range(NHD)]
        ssq = sb("ossq", [64, NCH]); b_ssq = B("ossq")
        R = 6
        ringA, ringB = [], []
        for ri in range(3):
            q = Ctx()
            q.gtri = sb("gtri%d" % ri, [64, 2, NHD, 64], BF16); q.b_gtri = B("gtri%d" % ri)
            q.Gsb = sb("Gsb%d" % ri, [128, NHD, 64]); q.b_Gsb = B("Gsb%d" % ri)
            q.EGbc = sb("EGbc%d" % ri, [128, NHD, 64]); q.b_EGbc = B("EGbc%d" % ri)
            q.Gc = sb("Gc%d" % ri, [64, NHD]); q.b_Gc = B("Gc%d" % ri)
            q.nEGc = sb("nEGc%d" % ri, [64, NHD]); q.b_nEGc = B("nEGc%d" % ri)
            q.Dm = sb("Dm%d" % ri, [64, NHD, 64]); q.b_Dm = B("Dm%d" % ri)
            q.decT = sb("decT%d" % ri, [64, NHD, 64]); q.b_decT = B("decT%d" % ri)
            q.Xf = sb("Xf%d" % ri, [64, NHD, 64], BF16); q.b_Xf = B("Xf%d" % ri)
            q.MT = sb("MT%d" % ri, [64, NHD, 64], BF16); q.b_MT = B("MT%d" % ri)
            ringA.append(q)
        for ri in range(2):
            q = Ctx()
            q.Xo = [sb("Xo%d_%d" % (ri, k), [64, NHD, 64], BF16) for k in range(3)]
            q.b_Xo = B("Xo%d" % ri)
            q.Tn = [sb("Tn%d_%d" % (ri, k), [64, NHD, 64], BF16) for k in range(2)]
            q.b_Tn = [B("Tn%d_%d" % (ri, k)) for k in range(2)]
            q.TTn = [sb("TTn%d_%d" % (ri, k), [64, NHD, 64], BF16) for k in range(2)]
            q.b_TTn = [B("TTn%d_%d" % (ri, k)) for k in range(2)]
            q.Mm = sb("Mm%d" % ri, [64, NHD, 64], BF16); q.b_Mm = B("Mm%d" % ri)
            q.X = [sb("X%d_%d" % (ri, k), [64, NHD, 64], BF16) for k in range(2)]
            q.b_X = [B("X%d_%d" % (ri, k)) for k in range(2)]
            q.XT = [sb("XT%d_%d" % (ri, k), [64, NHD, 64], BF16) for k in range(2)]
            q.b_XT = [B("XT%d_%d" % (ri, k)) for k in range(2)]
            q.P = [sb("P%d_%d" % (ri, k), [64, NHD, 64], BF16) for k in range(2)]
            q.b_P = [B("P%d_%d" % (ri, k)) for k in range(2)]
            q.qg = sb("qg%d" % ri, [128, NHD, 64], BF16); q.b_qg = B("qg%d" % ri)
            q.e2 = sb("e2%d" % ri, [64, NHD]); q.b_e2 = B("e2%d" % ri)
            q.kd = sb("kd%d" % ri, [64, NHD, 128], BF16); q.b_kd = B("kd%d" % ri)
            q.eGl = sb("eGl%d" % ri, [128, NHD]); q.b_eGl = B("eGl%d" % ri)
            ringB.append(q)
        ring = []
        for ri in range(6):
            q = Ctx()
            q.__dict__.update(ringA[ri % 3].__dict__)
            q.__dict__.update(ringB[ri % 2].__dict__)
            ring.append(q)
        r0 = sb("r0", [64, NHD, 128], BF16); b_r0 = [B("r0_%d" % i) for i in range(NHD)]
        vnew = sb("vnew", [64, NHD, 128], BF16); b_vnew = [B("vnew%d" % i) for i in range(NHD)]
        psG = st.enter_context(_pst(nc, "d_psG", [128, 512], F32)); b_psG = B("psG")
        psK = st.enter_context(_pst(nc, "d_psK", [128, 512], F32)); b_psK = B("psK")
        psN = st.enter_context(_pst(nc, "d_psN", [128, 512], F32)); b_psN = B("psN")
        psP = st.enter_context(_pst(nc, "d_psP", [128, 512], F32)); b_psP = B("psP")
        psT = st.enter_context(_pst(nc, "d_psT", [128, 512], F32)); b_psT = B("psT")
        psKS = st.enter_context(_pst(nc, "d_psKS", [128, 512], F32)); b_psKS = B("psKS")
        psVN = st.enter_context(_pst(nc, "d_psVN", [128, 512], F32)); b_psVN = B("psVN")
        psO = st.enter_context(_pst(nc, "d_psO", [128, 512], F32)); b_psO = B("psO")
        scan_ps = [(psKS, b_psKS), (psVN, b_psVN), (psO, b_psO), (psP, b_psP)]
        cm, cb = g.cm, g.b_const
        id64 = g.ident[0:64, 0:64]

        S.op("gpsimd", lambda e: e.memset(raw[:], 0.0), writes=[b_raw])
        cp(S, "gpsimd", cmb[:], cm[:], [cb], [b_nA])
        act(S, nA[:], g.dnc[:, l, 0, :], AF.Exp, [cb], [b_nA])
        ts(S, "gpsimd", nA[:], nA[:], -1.0, None, ALU.mult, None, [b_nA], [b_nA])

        def chunk_of(d, s):
            if d == 0:
                return s
            return 3 - s if s < 4 else 39 - s

        for b in range(g.NBL):
            if g.dn_phase == "0":
                break
            S.dma("sync", abraw[:], g.AB[b].rearrange("(c p) f -> p c f", p=64), writes=[b_abraw])
            tt(S, "vector", tmpa[:], abraw[:, :, 0:16], g.dnc[:, l, 1, :].unsqueeze(1).broadcast_to([64, NCH, 16]),
               ALU.add, [b_abraw, cb], [b_tmpa])
            ts(S, "vector", tmpa[:], tmpa[:], 30.0, None, ALU.min, None, [b_tmpa], [b_tmpa])
            act(S, tmpa[:], tmpa[:], AF.Exp, [b_tmpa], [b_tmpa])
            act(S, tmpa[:], tmpa[:], AF.Ln, [b_tmpa, cb], [b_tmpa], bias=g.cvals[0:64, 2:3])
            tt(S, "vector", gg[:], tmpa[:], nA[:].unsqueeze(1).broadcast_to([64, NCH, 16]), ALU.mult,
               [b_tmpa, b_nA], [b_gg])
            cp(S, "gpsimd", ghi[:], gg[:], [b_gg], [b_gg])
            tt(S, "gpsimd", glo[:], gg[:], ghi[:], ALU.subtract, [b_gg], [b_gg])
            act(S, beta[:], abraw[:, :, 16:32], AF.Exp, [b_abraw], [b_beta], scale=-1.0)
            ts(S, "vector", beta[:], beta[:], 1.0, None, ALU.add, None, [b_beta], [b_beta])
            S.op("vector", lambda e: e.reciprocal(out=beta[:], in_=beta[:]), [b_beta], [b_beta])
            for hg in range(8 // HG):
                h0 = hg * HG
                for sel in range(3):
                    dst, b_dst = ((qT, b_qT), (kT, b_kT), (vT, b_vT))[sel]
                    for hl in range(HG):
                        jj = sel * 8 + h0 + hl
                        S.dma("sync", raw[:, 1:1 + CTXL], g.PQ[b, jj, :, 0:CTXL], parts=[b_raw])
                        S.dma("sync", raw[:, 2 + CTXL:2 + T], g.PQ[b, jj, :, CTXL:T], parts=[b_raw])
                        w = lambda k: g.vecs[:, l, V_SH + jj * 3 + k:V_SH + jj * 3 + k + 1]
                        ts(S, "vector", y[:, 1:TP - 1], raw[:, 1:TP - 1], w(1), None, ALU.mult, None, [b_raw, cb], [b_y])
                        stt(S, y[:, 1:TP - 1], raw[:, 0:TP - 2], w(0), y[:, 1:TP - 1], ALU.mult, ALU.add, [b_raw, cb, b_y], [b_y])
                        stt(S, y[:, 1:TP - 1], raw[:, 2:TP], w(2), y[:, 1:TP - 1], ALU.mult, ALU.add, [b_raw, cb, b_y], [b_y])
                        act(S, t[:, 1:TP - 1], y[:, 1:TP - 1], AF.Tanh, [b_y], [b_t], scale=0.5)
                        stt(S, t[:, 1:TP - 1], t[:, 1:TP - 1], 1.0, y[:, 1:TP - 1], ALU.add, ALU.mult, [b_y, b_t], [b_t])
                        if sel == 2:
                            ts(S, "gpsimd", vT[:, hl, 1:TP - 1], t[:, 1:TP - 1], 0.5, None, ALU.mult, None, [b_t], [b_vT])
                            continue
                        tt(S, "gpsimd", sqb[:, 1:TP - 1], t[:, 1:TP - 1], t[:, 1:TP - 1], ALU.mult, [b_t], [b_sqb])
                        pos = 1
                        pi = 0
                        while pos < TP - 1:
                            n = min(512, TP - 1 - pos)
                            ps_, bps_ = scan_ps[pi % 4]
                            mm(S, ps_[:, 0:n], g.ones_bf[:], sqb[:, pos:pos + n], True, True, [b_sqb, cb], [bps_])
                            act(S, y[:, pos:pos + n], ps_[:, 0:n], AF.Ln, [bps_, cb], [b_y], bias=g.cvals[:, 1:2])
                            pos += n
                            pi += 1
                        act(S, y[:, 1:TP - 1], y[:, 1:TP - 1], AF.Exp, [b_y], [b_y], scale=-0.5)
                        stt(S, dst[:, hl, 1:TP - 1], t[:, 1:TP - 1], (128.0 ** -0.5) if sel == 0 else 1.0, y[:, 1:TP - 1],
                            ALU.mult, ALU.mult, [b_t, b_y], [b_dst])
                for (srcT, b_src, dstk, b_dk) in ((kT, b_kT, ktok, b_ktok), (vT, b_vT, vtok, b_vtok)):
                    for hl in range(HG):
                        for c0 in range(0, NCH, 4):
                            ncg = min(4, NCH - c0)
                            for ci in range(ncg):
                                c = c0 + ci
                                tr(S, psT[0:64, ci * 128:(ci + 1) * 128], srcT[:, hl, ptok(c):ptok(c) + 64], g.ident[:],
                                   [b_src, cb], [b_psT])
                            cp(S, "scalar" if (c0 // 4) % 2 == 0 else "vector", dstk[:, c0:c0 + ncg, hl, :],
                               psT[0:64, 0:ncg * 128].rearrange("p (c f) -> p c f", f=128), [b_psT], [b_dk])
                S.op("gpsimd", lambda e: e.memset(oacc[:], 0.0), writes=[b_oacc])
                for i in range(NHD):
                    S.op("gpsimd", lambda e, i=i: e.memset(Sst[:, i, :], 0.0), writes=[b_S[i]])
                    S.op("gpsimd", lambda e, i=i: e.memset(Sbf[:, i, :], 0.0), writes=[b_Sbf[i]])

                def B_pre(s):
                    q = ring[s % R]
                    cs_ = [chunk_of(0, s), chunk_of(1, s)]
                    for d in range(2):
                        for hi_, gsrc in enumerate((ghi, glo)):
                            gsl = gsrc[:, cs_[d], d * 8 + h0:d * 8 + h0 + HG]
                            tt(S, "gpsimd", q.gtri[:, hi_, d * HG:(d + 1) * HG, :], gsl.unsqueeze(2).broadcast_to([64, HG, 64]),
                               cmb[:, d, :].unsqueeze(1).broadcast_to([64, HG, 64]), ALU.mult, [b_gg, b_nA], [q.b_gtri])
                    for hi_ in range(2):
                        mm(S, psG[:, 0:NHD * 64], ones64, q.gtri[:, hi_].rearrange("p a b -> p (a b)"), hi_ == 0, hi_ == 1,
                           [q.b_gtri, cb], [b_psG])
                    for d in range(2):
                        for hi_, gsrc in enumerate((ghi, glo)):
                            mm(S, psG[0:64, 256 + d * HG:256 + (d + 1) * HG], cmb[:, d, :],
                               gsrc[:, cs_[d], d * 8 + h0:d * 8 + h0 + HG], hi_ == 0, hi_ == 1, [b_gg, b_nA], [b_psG])
                    cp(S, "scalar", q.Gsb[:].rearrange("p a b -> p (a b)"), psG[:, 0:NHD * 64], [b_psG], [q.b_Gsb])
                    act(S, q.EGbc[:].rearrange("p a b -> p (a b)"), psG[:, 0:NHD * 64], AF.Exp, [b_psG], [q.b_EGbc])
                    cp(S, "scalar", q.Gc[:], psG[0:64, 256:256 + NHD], [b_psG], [q.b_Gc])
                    act(S, q.nEGc[:], psG[0:64, 256:256 + NHD], AF.Exp, [b_psG], [q.b_nEGc])
                    ts(S, "gpsimd", q.nEGc[:], q.nEGc[:], -1.0, None, ALU.mult, None, [q.b_nEGc], [q.b_nEGc])
                    tt(S, "vector", q.Dm[:], q.Gsb[0:64], q.Gc[:].unsqueeze(2).broadcast_to([64, NHD, 64]), ALU.subtract,
                       [q.b_Gsb, q.b_Gc], [q.b_Dm])
                    for d in range(2):
                        stt(S, q.Dm[:, d * HG:(d + 1) * HG, :], q.Dm[:, d * HG:(d + 1) * HG, :], 0.0,
                            cm[:, 2 + d, :].unsqueeze(1).broadcast_to([64, HG, 64]), ALU.min, ALU.add,
                            [q.b_Dm, cb], [q.b_Dm])
                    act(S, q.decT[:], q.Dm[:], AF.Exp, [q.b_Dm], [q.b_decT])
                    for d in range(2):
                        for hl in range(HG):
                            dh = d * HG + hl
                            ksl = kT[:, hl, ptok(cs_[d]):ptok(cs_[d]) + 64]
                            mm(S, psK[0:64, dh * 64:(dh + 1) * 64], ksl, ksl, True, True, [b_kT], [b_psK])
                            mm(S, psK[0:64, 256 + dh * 64:256 + (dh + 1) * 64], ksl,
                               qT[:, hl, ptok(cs_[d]):ptok(cs_[d]) + 64], True, True, [b_kT, b_qT], [b_psK])
                    for d in range(2):
                        for hl in range(HG):
                            dh = d * HG + hl
                            stt(S, q.Xf[:, dh, :], psK[0:64, dh * 64:(dh + 1) * 64],
                                beta[:, cs_[d], d * 8 + h0 + hl:d * 8 + h0 + hl + 1], q.decT[:, dh, :], ALU.mult, ALU.mult,
                                [b_psK, b_beta, q.b_decT], [q.b_Xf])
                    tt(S, "vector", q.MT[:], psK[0:64, 256:512].rearrange("p (a b) -> p a b", b=64), q.decT[:], ALU.mult,
                       [b_psK, q.b_decT], [q.b_MT])

                NLEV = 2

                def B_neu(s, k):
                    q = ring[s % R]
                    mk = lambda m: cmb[:, m, :].unsqueeze(1).broadcast_to([64, NHD, 64])
                    if k == 0:
                        tt(S, "gpsimd", q.X[0][:], q.Xf[:], mk(6), ALU.mult, [q.b_Xf, b_nA], [q.b_X[0]])
                        for lv in range(3):
                            tt(S, "gpsimd", q.Xo[lv][:], q.Xf[:], mk(7 + lv), ALU.mult, [q.b_Xf, b_nA], [q.b_Xo])
                        for dh in range(NHD):
                            tr(S, psT[0:64, dh * 64:(dh + 1) * 64], q.X[0][:, dh, :], id64, [q.b_X[0], cb], [b_psT])
                        cp(S, "scalar", q.XT[0][:].rearrange("p a b -> p (a b)"), psT[0:64, 0:NHD * 64], [b_psT], [q.b_XT[0]])
                        stt(S, q.P[0][:], q.X[0][:], -1.0, id64.unsqueeze(1).broadcast_to([64, NHD, 64]), ALU.mult, ALU.add,
                            [q.b_X[0], cb], [q.b_P[0]])
                        return
                    if k <= NLEV + 1:
                        pv, cu = (k - 1) % 2, k % 2
                        if k <= NLEV:
                            for dh in range(NHD):
                                if k < NLEV:
                                    mm(S, psN[0:64, dh * 64:(dh + 1) * 64], q.XT[pv][:, dh, :], q.X[pv][:, dh, :], True, True,
                                       [q.b_XT[pv], q.b_X[pv]], [b_psN])
                                mm(S, psN[0:64, 256 + dh * 64:256 + (dh + 1) * 64], q.X[pv][:, dh, :], q.XT[pv][:, dh, :], True, True,
                                   [q.b_XT[pv], q.b_X[pv]], [b_psN])
                        if k >= 2:
                            for dh in range(NHD):
                                mm(S, psP[0:64, dh * 64:(dh + 1) * 64], q.XT[pv][:, dh, :], q.P[k % 2][:, dh, :], True, True,
                                   [q.b_XT[pv], q.b_P[k % 2]], [b_psP])
                        if k <= NLEV:
                            if k < NLEV:
                                cp(S, "scalar", q.X[cu][:].rearrange("p a b -> p (a b)"), psN[0:64, 0:256], [b_psN], [q.b_X[cu]])
                            cp(S, "scalar", q.XT[cu][:].rearrange("p a b -> p (a b)"), psN[0:64, 256:512], [b_psN], [q.b_XT[cu]])
                        if k >= 2:
                            tt(S, "vector", q.P[(k + 1) % 2][:].rearrange("p a b -> p (a b)"),
                               q.P[k % 2][:].rearrange("p a b -> p (a b)"), psP[0:64, 0:256], ALU.add,
                               [q.b_P[k % 2], b_psP], [q.b_P[(k + 1) % 2]])
                        return
                    fin = (NLEV + 2) % 2
                    if k == NLEV + 2:
                        for dh in range(NHD):
                            tr(S, psT[0:64, dh * 64:(dh + 1) * 64], q.P[fin][:, dh, :], id64, [q.b_P[fin], cb], [b_psT])
                        cp(S, "scalar", q.Tn[0][:].rearrange("p a b -> p (a b)"), psT[0:64, 0:NHD * 64], [b_psT], [q.b_Tn[0]])
                        return
                    kk_ = k - (NLEV + 3)
                    lv, ph = kk_ // 2, kk_ % 2
                    Tc, b_Tc = q.Tn[lv % 2], q.b_Tn[lv % 2]
                    Tx, b_Tx = q.Tn[(lv + 1) % 2], q.b_Tn[(lv + 1) % 2]
                    if lv == 0:
                        TTc, b_TTc = q.P[fin], q.b_P[fin]
                    else:
                        TTc, b_TTc = q.TTn[lv % 2], q.b_TTn[lv % 2]
                    TTx, b_TTx = q.TTn[(lv + 1) % 2], q.b_TTn[(lv + 1) % 2]
                    if ph == 0:
                        for dh in range(NHD):
                            mm(S, psN[0:64, dh * 64:(dh + 1) * 64], q.Xo[lv][:, dh, :], Tc[:, dh, :], True, True,
                               [q.b_Xo, b_Tc], [b_psN])
                        cp(S, "scalar", q.Mm[:].rearrange("p a b -> p (a b)"), psN[0:64, 0:256], [b_psN], [q.b_Mm])
                    else:
                        for dh in range(NHD):
                            if lv < 2:
                                mm(S, psP[0:64, dh * 64:(dh + 1) * 64], TTc[:, dh, :], q.Mm[:, dh, :], True, True,
                                   [b_TTc, q.b_Mm], [b_psP])
                            mm(S, psP[0:64, 256 + dh * 64:256 + (dh + 1) * 64], q.Mm[:, dh, :], TTc[:, dh, :], True, True,
                               [b_TTc, q.b_Mm], [b_psP])
                        if lv < 2:
                            tt(S, "vector", Tx[:].rearrange("p a b -> p (a b)"), Tc[:].rearrange("p a b -> p (a b)"),
                               psP[0:64, 0:256], ALU.subtract, [b_Tc, b_psP], [b_Tx])
                        tt(S, "vector", TTx[:].rearrange("p a b -> p (a b)"), TTc[:].rearrange("p a b -> p (a b)"),
                           psP[0:64, 256:512], ALU.subtract, [b_TTc, b_psP], [b_TTx])

                NB_ = NLEV + 3 + 6

                def B_post(s):
                    q = ring[s % R]
                    cs_ = [chunk_of(0, s), chunk_of(1, s)]
                    for d in range(2):
                        last = 63 if d == 0 else 0
                        sl = slice(d * HG, (d + 1) * HG)
                        tt(S, "gpsimd", q.qg[:, sl, :], qT[:, :, ptok(cs_[d]):ptok(cs_[d]) + 64], q.EGbc[:, sl, :], ALU.mult,
                           [b_qT, q.b_EGbc], [q.b_qg])
                        tt(S, "gpsimd", q.e2[:, sl], q.Gsb[0:64, sl, last], q.Gc[:, sl], ALU.subtract, [q.b_Gsb, q.b_Gc], [q.b_e2])
                        cp(S, "gpsimd", q.eGl[:, sl], q.EGbc[:, sl, last], [q.b_EGbc], [q.b_eGl])
                    act(S, q.e2[:], q.e2[:], AF.Exp, [q.b_e2], [q.b_e2])
                    for d in range(2):
                        sl = slice(d * HG, (d + 1) * HG)
                        tt(S, "gpsimd", q.kd[:, sl, :], ktok[:, cs_[d], :, :], q.e2[:, sl].unsqueeze(2).broadcast_to([64, HG, 128]),
                           ALU.mult, [b_ktok, q.b_e2], [q.b_kd])

                def C_step(s, part):
                    q = ring[s % R]
                    TTm, b_TT = q.TTn[1], q.b_TTn[1]
                    for d in range(2):
                        c = chunk_of(d, s)
                        for hl in range(HG):
                            dh = d * HG + hl
                            bcol = d * 8 + h0 + hl
                            ksl = kT[:, hl, ptok(c):ptok(c) + 64]
                            if part == 0:
                                mm(S, psKS[0:64, dh * 128:(dh + 1) * 128], ksl, Sbf[:, dh, :], True, True,
                                   [b_kT, b_Sbf[dh]], [b_psKS])
                                stt(S, r0[:, dh, :], psKS[0:64, dh * 128:(dh + 1) * 128], q.nEGc[:, dh:dh + 1],
                                    vtok[:, c, hl, :], ALU.mult, ALU.add, [b_psKS, q.b_nEGc, b_vtok], [b_r0[dh]])
                            elif part == 1:
                                mm(S, psVN[0:64, dh * 128:(dh + 1) * 128], TTm[:, dh, :], r0[:, dh, :], True, True,
                                   [b_TT, b_r0[dh]], [b_psVN])
                                act(S, vnew[:, dh, :], psVN[0:64, dh * 128:(dh + 1) * 128], AF.Identity,
                                    [b_psVN, b_beta], [b_vnew[dh]], scale=beta[:, c, bcol:bcol + 1])
                            elif part == 2:
                                mm(S, psO[0:64, dh * 128:(dh + 1) * 128], q.MT[:, dh, :], vnew[:, dh, :], True, False,
                                   [q.b_MT, b_vnew[dh]], [b_psO])
                                mm(S, psO[0:64, dh * 128:(dh + 1) * 128], q.qg[:, dh, :], Sbf[:, dh, :], False, True,
                                   [q.b_qg, b_Sbf[dh]], [b_psO])
                                tt(S, "vector", oacc[:, c, hl, :], oacc[:, c, hl, :], psO[0:64, dh * 128:(dh + 1) * 128], ALU.add,
                                   [b_psO, b_oacc], [b_oacc])
                            else:
                                mm(S, psKS[:, dh * 128:(dh + 1) * 128], q.kd[:, dh, :], vnew[:, dh, :], True, True,
                                   [q.b_kd, b_vnew[dh]], [b_psKS])
                                stt(S, Sst[:, dh, :], Sst[:, dh, :], q.eGl[:, dh:dh + 1], psKS[:, dh * 128:(dh + 1) * 128],
                                    ALU.mult, ALU.add, [b_psKS, q.b_eGl, b_S[dh]], [b_S[dh]])
                                cp(S, "gpsimd", Sbf[:, dh, :], Sst[:, dh, :], [b_S[dh]], [b_Sbf[dh]])

                if g.dn_phase == "A":
                    S.dma("sync", g.OT[b, :, h0, :], kT[:, 0, 1:1 + T], reads=[b_kT], sem_buf=b_kT)
                    S.dma("sync", g.OT[b, :, h0 + 1, :], vT[:, 0, 1:1 + T], reads=[b_vT], sem_buf=b_vT)
                    break
                nsteps = NCH if g.dn_phase is None else int(g.dn_phase)
                B_pre(0)
                if nsteps > 1:
                    B_pre(1)
                for k in range(NB_):
                    B_neu(0, k)
                B_post(0)
                sched = [(0, 1), (1, 3), (2, 6), (3, 9)]
                for s in range(nsteps):
                    nxt = s + 1 < nsteps
                    if s + 2 < nsteps:
                        B_pre(s + 2)
                    kb = 0
                    for part, upto in sched:
                        if nxt:
                            while kb < upto:
                                B_neu(s + 1, kb)
                                kb += 1
                        C_step(s, part)
                    if nxt:
                        while kb < NB_:
                            B_neu(s + 1, kb)
                            kb += 1
                        B_post(s + 1)

                if g.dn_phase is not None and g.dn_phase != "36":
                    break
                otmp = vT[0:64].rearrange("p a b -> p (a b)")[:, 0:NCH * 128].rearrange("p (c f) -> p c f", f=128)
                oT = sqb
                for hl in range(HG):
                    tt(S, "gpsimd", otmp, oacc[:, :, hl, :], oacc[:, :, hl, :], ALU.mult, [b_oacc], [b_vT])
                    S.op("vector", lambda e: e.reduce_sum(out=ssq[:], in_=otmp, axis=AX.X), [b_vT], [b_ssq])
                    act(S, ssq[:], ssq[:], AF.Ln, [b_ssq, cb], [b_ssq], scale=1.0 / 128.0, bias=g.cvals[0:64, 0:1])
                    act(S, ssq[:], ssq[:], AF.Exp, [b_ssq], [b_ssq], scale=-0.5)
                    tt(S, "vector", otmp, oacc[:, :, hl, :], ssq[:].unsqueeze(2).broadcast_to([64, NCH, 128]), ALU.mult,
                       [b_oacc, b_ssq], [b_vT])
                    for c0 in range(0, NCH, 8):
                        ncg = min(8, NCH - c0)
                        for ci in range(ncg):
                            tr(S, psT[:, ci * 64:(ci + 1) * 64], otmp[:, c0 + ci, :], id64, [b_vT, cb], [b_psT])
                        cp(S, "scalar" if (c0 // 8) % 2 == 0 else "vector", oT[:, c0 * 64:(c0 + ncg) * 64], psT[:, 0:ncg * 64],
                           [b_psT], [b_sqb])
                    S.dma("sync", g.OT[b, :, h0 + hl, :], oT[:, 0:T], reads=[b_sqb], sem_buf=b_sqb)
        S.barrier()
        S.flush()
        S.release([b_raw, b_abraw, b_sqb])


def stage_m3(g, l, with_ctx):
    nc, S = g.nc, g.S
    with ExitStack() as st:
        sb = lambda n, s, d=F32: st.enter_context(_sbt(nc, "e_" + n, list(s), d))
        WZ = sb("wz", [128, KC, D], BF16)
        WG = sb("wg", [128, KC, 2 * D], BF16)
        DP = sb("dp", [128, KC, D], BF16)
        WO = sb("wo", [128, KC, D], BF16)
        b_wz, b_wg, b_dp, b_wo = Buf("wz"), Buf("wg"), Buf("dp"), Buf("wo")
        load_w(S, WZ[:], g.win[l][:, :, O_Z:O_AB], b_wz, 4, axis=1)
        load_w(S, WG[:], g.win[l][:, :, O_GATE:NIN], b_wg, 8, axis=1)
        load_w(S, DP[:], g.dproj[l], b_dp, 4, axis=1)
        load_w(S, WO[:], g.wout[l], b_wo, 4, axis=1)
        hb = [sb("h%d" % i, [128, KC, TN]) for i in range(2)]
        b_h = [Buf("h0"), Buf("h1")]
        otb = [sb("ot%d" % i, [128, KC, TN], BF16) for i in range(2)]
        b_ot = [Buf("ot0"), Buf("ot1")]
        ycb = [sb("yc%d" % i, [128, KC, TN], BF16) for i in range(2)]
        b_yc = [Buf("yc0"), Buf("yc1")]
        r = alloc_norm(g, st, "e_")
        sz = [sb("sz%d" % i, [128, TN]) for i in range(2)]
        b_sz = [Buf("sz0"), Buf("sz1")]
        og = sb("og", [128, KC, TN], BF16)
        b_og = Buf("og")
        tA = [sb("tA%d" % i, [128, TN]) for i in range(2)]
        tB = [sb("tB%d" % i, [128, TN]) for i in range(2)]
        mA = [sb("mA%d" % i, [128, TN]) for i in range(2)]
        mB = [sb("mB%d" % i, [128, TN]) for i in range(2)]
        b_tA = [Buf("tA0"), Buf("tA1")]
        b_tB = [Buf("tB0"), Buf("tB1")]
        b_mA = [Buf("mA0"), Buf("mA1")]
        b_mB = [Buf("mB0"), Buf("mB1")]
        mt = sb("m", [128, KC, TN], BF16)
        b_m = Buf("m")
        psZ = [st.enter_context(_pst(nc, "e_psZ%d" % i, [128, 512], F32)) for i in range(2)]
        psD = st.enter_context(_pst(nc, "e_psD", [128, 512], F32))
        psA = st.enter_context(_pst(nc, "e_psA", [128, 512], F32))
        psB = st.enter_context(_pst(nc, "e_psB", [128, 512], F32))
        psO = [st.enter_context(_pst(nc, "e_psO%d" % i, [128, 512], F32)) for i in range(2)]
        b_psZ = [Buf("epsZ0"), Buf("epsZ1")]
        b_psD, b_psA, b_psB = Buf("epsD"), Buf("epsA"), Buf("epsB")
        b_psO = [Buf("epsO0"), Buf("epsO1")]
        tiles = tile_list(g, with_ctx)

        def load(i):
            b, ti = tiles[i]
            sl = slice(ti * TN, (ti + 1) * TN)
            S.dma("sync", hb[i % 2][:], g.H[b, :, :, sl], writes=[b_h[i % 2]])
            S.dma("sync", otb[i % 2][:], g.OT[b, :, :, sl], writes=[b_ot[i % 2]])
            S.dma("sync", ycb[i % 2][:], g.YC[b, :, :, sl], writes=[b_yc[i % 2]])

        load(0)
        for i, (b, ti) in enumerate(tiles):
            h, ot, yc = hb[i % 2], otb[i % 2], ycb[i % 2]
            bh, bot, byc = b_h[i % 2], b_ot[i % 2], b_yc[i % 2]
            col = 4 if ti == 0 else b
            if i + 1 < len(tiles):
                load(i + 1)
            norm_mod(g, r, h, bh, l, 3, 4, col)
            for hh in range(KC):
                pz = psZ[hh % 2]
                for kc in range(KC):
                    mm(S, pz[:, 0:TN], WZ[:, kc, hh * 128:(hh + 1) * 128], r.n[:, kc, :], kc == 0, kc == KC - 1,
                       [b_wz, r.b_n], [b_psZ[hh % 2]])
                act(S, sz[hh % 2][:], pz[:, 0:TN], AF.Silu, [b_psZ[hh % 2]], [b_sz[hh % 2]])
                stt(S, og[:, hh, :], sz[hh % 2][:], g.vecs[:, l, V_ON:V_ON + 1], ot[:, hh, :], ALU.mult, ALU.mult,
                    [b_sz[hh % 2], bot, g.b_const], [b_og])
            for dc in range(KC):
                for hh in range(KC):
                    mm(S, psD[:, 0:TN], DP[:, hh, dc * 128:(dc + 1) * 128], og[:, hh, :], hh == 0, hh == KC - 1,
                       [b_dp, b_og], [b_psD])
                for kc in range(KC):
                    mm(S, psA[:, 0:TN], WG[:, kc, dc * 128:(dc + 1) * 128], r.n[:, kc, :], kc == 0, kc == KC - 1,
                       [b_wg, r.b_n], [b_psA])
                for kc in range(KC):
                    mm(S, psB[:, 0:TN], WG[:, kc, D + dc * 128:D + (dc + 1) * 128], r.n[:, kc, :], kc == 0, kc == KC - 1,
                       [b_wg, r.b_n], [b_psB])
                p = dc % 2
                act(S, tA[p][:], psA[:, 0:TN], AF.Tanh, [b_psA], [b_tA[p]], scale=0.5)
                act(S, tB[p][:], psB[:, 0:TN], AF.Tanh, [b_psB], [b_tB[p]], scale=0.5)
                stt(S, mA[p][:], tA[p][:], 1.0, yc[:, dc, :], ALU.add, ALU.mult, [b_tA[p], byc], [b_mA[p]])
                stt(S, mB[p][:], tB[p][:], 1.0, psD[:, 0:TN], ALU.add, ALU.mult, [b_tB[p], b_psD], [b_mB[p]])
                tt(S, "gpsimd", mt[:, dc, :], mA[p][:], mB[p][:], ALU.add, [b_mA[p], b_mB[p]], [b_m])
            for dc in range(KC):
                po = psO[dc % 2]
                for c in range(KC):
                    mm(S, po[:, 0:TN], WO[:, c, dc * 128:(dc + 1) * 128], mt[:, c, :], c == 0, c == KC - 1,
                       [b_wo, b_m], [b_psO[dc % 2]])
                stt(S, h[:, dc, :], po[:, 0:TN], g.mods[:, l, 5 * 8 + dc, col:col + 1], h[:, dc, :], ALU.mult, ALU.add,
                    [b_psO[dc % 2], bh], [bh])
            S.dma("sync", g.H[b, :, :, ti * TN:(ti + 1) * TN], h[:], reads=[bh], sem_buf=bh)
        S.barrier()
        S.flush()
        S.release([b_wz, b_wg, b_dp, b_wo] + b_h + b_ot + b_yc)


def stage_final(g):
    nc, S = g.nc, g.S
    with ExitStack() as st:
        sb = lambda n, s, d=F32: st.enter_context(_sbt(nc, "z_" + n, list(s), d))
        hb = [sb("h%d" % i, [128, KC, TN]) for i in range(2)]
        b_h = [Buf("h0"), Buf("h1")]
        ob = [sb("o%d" % i, [128, KC, TN]) for i in range(2)]
        b_o = [Buf("o0"), Buf("o1")]
        r = alloc_norm(g, st, "z_")
        tiles = tile_list(g, False)

        def load(i):
            b, ti = tiles[i]
            S.dma("sync", hb[i % 2][:], g.H[b, :, :, ti * TN:(ti + 1) * TN], writes=[b_h[i % 2]])

        load(0)
        for i, (b, ti) in enumerate(tiles):
            h, bh = hb[i % 2], b_h[i % 2]
            if i + 1 < len(tiles):
                load(i + 1)
            tt(S, "gpsimd", r.sq[:], h[:], h[:], ALU.mult, [bh], [r.b_sq])
            for c in range(KC):
                mm(S, r.pss[:, 0:TN], g.ones_bf[:], r.sq[:, c, :], c == 0, c == KC - 1, [r.b_sq, g.b_const], [r.b_pss])
            act(S, r.lr[:], r.pss[:, 0:TN], AF.Ln, [r.b_pss, g.b_const], [r.b_lr], scale=1.0 / D, bias=g.cvals[:, 0:1])
            act(S, r.rstd[:], r.lr[:], AF.Exp, [r.b_lr], [r.b_rstd], scale=-0.5)
            tt(S, "vector", r.u[:], h[:], r.rstd[:].unsqueeze(1).broadcast_to([128, KC, TN]), ALU.mult,
               [bh, r.b_rstd], [r.b_u])
            o = ob[i % 2]
            for c in range(KC):
                act(S, o[:, c, :], r.u[:, c, :], AF.Identity, [r.b_u, g.b_const], [b_o[i % 2]], scale=g.fin[:, c:c + 1])
            S.dma("sync", g.outT[b, :, :, (ti - 1) * TN:ti * TN], o[:], reads=[b_o[i % 2]], sem_buf=b_o[i % 2])
        S.barrier()
        S.flush()
        S.release(b_h + b_o)


def fm(v):
    v = np.asarray(v, np.float32)
    return np.ascontiguousarray(v.reshape(-1, 128).T)


def wl(w):
    w = np.asarray(w, np.float32)
    K, N = w.shape
    return np.ascontiguousarray(w.reshape(K // 128, 128, N).transpose(1, 0, 2))


def make_consts():
    i = np.arange(64)
    cm = np.zeros((64, 10, 64), np.float32)
    jj, ii = i[:, None], i[None, :]
    cm[:, 6, :] = (jj // 8 == ii // 8) & (jj != ii)
    for lv, sz in enumerate((8, 16, 32)):
        cm[:, 7 + lv, :] = (jj // (2 * sz) == ii // (2 * sz)) & (jj // sz != ii // sz)
    cm[:, 0, :] = (i[:, None] <= i[None, :])
    cm[:, 1, :] = (i[:, None] >= i[None, :])
    cm[:, 2, :] = np.where(i[None, :] >= i[:, None], 0.0, -1e30)
    cm[:, 3, :] = np.where(i[None, :] <= i[:, None], 0.0, -1e30)
    cm[:, 4, :] = (i[None, :] > i[:, None])
    cm[:, 5, :] = (i[None, :] < i[:, None])
    return np.eye(128, dtype=np.float32), cm


def prep_shared(inp, L):
    sh = {}
    sh["adaw"] = np.stack([wl(inp["ada_w"][l]) for l in range(L)])
    sh["adab"] = np.stack([fm(inp["ada_b"][l]) for l in range(L)])
    vecs = np.zeros((128, L, NV), np.float32)
    for l in range(L):
        vecs[:, l, V_N1:V_N1 + 8] = fm(inp["ffn1_norm"][l])
        vecs[:, l, V_NM:V_NM + 8] = fm(inp["mix_norm"][l])
        vecs[:, l, V_N2:V_N2 + 8] = fm(inp["ffn2_norm"][l])
        vecs[:, l, V_CB:V_CB + 8] = fm(inp["conv_dw_b"][l])
        vecs[:, l, V_LG:V_LG + 8] = fm(inp["conv_ln_g"][l])
        vecs[:, l, V_LB:V_LB + 8] = fm(inp["conv_ln_b"][l])
        vecs[:, l, V_ON] = np.asarray(inp["dn_onorm"][l], np.float32)
        dw = np.asarray(inp["conv_dw"][l], np.float32)
        vecs[:, l, V_DW:V_DW + 248] = dw.reshape(31, 8, 128).transpose(2, 1, 0).reshape(128, 248)
        shw = np.asarray(inp["dn_short"][l], np.float32)
        vecs[:, l, V_SH:V_SH + 72] = shw.reshape(3, 24, 128).transpose(2, 1, 0).reshape(128, 72)
    sh["vecs"] = vecs
    sh["fin"] = fm(inp["final_norm"])
    dnc = np.zeros((64, L, 2, 16), np.float32)
    for l in range(L):
        dnc[:, l, 0, :] = np.asarray(inp["dn_a_log"][l], np.float32).reshape(1, 16)
        dnc[:, l, 1, :] = np.asarray(inp["dn_dt_bias"][l], np.float32).reshape(1, 16)
    sh["dnc"] = dnc
    sh["w13a"] = np.stack([wl(inp["ffn1_w13"][l]) for l in range(L)])
    sh["w13b"] = np.stack([wl(inp["ffn2_w13"][l]) for l in range(L)])
    sh["w2a"] = np.stack([wl(inp["ffn1_w2"][l]) for l in range(L)])
    sh["w2b"] = np.stack([wl(inp["ffn2_w2"][l]) for l in range(L)])
    sh["win"] = np.stack([wl(inp["w_in"][l]) for l in range(L)])
    sh["cproj"] = np.stack([wl(inp["conv_proj"][l]) for l in range(L)])
    sh["dproj"] = np.stack([wl(inp["dn_proj"][l]) for l in range(L)])
    sh["wout"] = np.stack([wl(inp["w_out"][l]) for l in range(L)])
    sh["ident"], sh["cmask"] = make_consts()
    return sh


def prep_core(inp, b0, NBL):
    x = np.asarray(inp["x"][b0:b0 + NBL], np.float32)
    cx = np.asarray(inp["ctx"][b0:b0 + NBL], np.float32)
    hcat = np.concatenate([cx, x], axis=1)
    hT0 = np.ascontiguousarray(hcat.reshape(NBL, T, KC, 128).transpose(0, 3, 2, 1))
    cT = np.zeros((128, KC, 5), np.float32)
    cc = np.asarray(inp["c"][b0:b0 + NBL], np.float32)
    cT[:, :, 0:NBL] = cc.reshape(NBL, KC, 128).transpose(2, 1, 0)
    cT[:, :, 4] = np.asarray(inp["c_ctx"], np.float32).reshape(KC, 128).T
    return {"hT0": hT0, "cT": cT}


_PROG = {}
NB_LAUNCH = 4


def kernel(**inputs):
    L, NBL = DEPTH, NB_LAUNCH
    per_core = BATCH // NCORES
    if "nc" not in _PROG:
        _PROG["nc"] = build_program(L, NBL)
    nc = _PROG["nc"]
    sh = prep_shared(inputs, L)
    out = np.zeros((BATCH, SEQ, D), np.float32)
    for r in range(per_core // NBL):
        in_maps = []
        for c in range(NCORES):
            m = dict(sh)
            m.update(prep_core(inputs, c * per_core + r * NBL, NBL))
            in_maps.append(m)
        res = run_bass_kernel_spmd(nc, in_maps, core_ids=list(range(NCORES)))
        for c in range(NCORES):
            oT = np.asarray(res.results[c]["outT"])
            b0 = c * per_core + r * NBL
            out[b0:b0 + NBL] = oT.transpose(0, 3, 2, 1).reshape(NBL, SEQ, D)
    return out
```
